# Optimizing a Trainium2 kernel written in Bass

```python
import jax, jax.numpy as jnp
from jax import lax
import numpy as np

D_MODEL = 2048
BATCH = 4
SEQ = 2048
DEPTH = 4

MLA_HEADS = 8
MLA_Q_LORA = 512
MLA_KV_LORA = 512
MLA_NOPE = 128
MLA_ROPE = 64
MLA_V = 128
MLA_QK = MLA_NOPE + MLA_ROPE
MLSTM_HEADS = 4
MLSTM_DK = 128
MLSTM_DV = 256
MLSTM_CHUNK = 128
CONV_WIDTH = 4
FORGET_BIAS = 3.0
DIL_HEADS = 16
DIL_HEAD_DIM = 128
DIL_PATTERNS = ((128, 1), (512, 4), (2048, 16))
DIL_BLOCK = 128
Q_BLOCK = 128
D_FF = 4 * D_MODEL
ROPE_THETA = 10000.0
NORM_EPS = 1e-6
N_EVEN = (DEPTH + 1) // 2
N_ODD = DEPTH // 2
EVEN_SPLIT_SIZES = (MLA_Q_LORA, MLA_KV_LORA, MLA_ROPE, 2 * MLSTM_HEADS * MLSTM_DK,
                    MLSTM_HEADS * MLSTM_DV, MLSTM_HEADS, MLSTM_HEADS, MLSTM_HEADS * MLSTM_DV)
EVEN_IN = sum(EVEN_SPLIT_SIZES)
EVEN_MIX = MLA_HEADS * MLA_V + MLSTM_HEADS * MLSTM_DV
ODD_MIX = DIL_HEADS * DIL_HEAD_DIM

kernel_name = 'hybrid_mla_mlstm_dilated_trunk'


def rms_norm(x, g):
    x32 = x.astype(jnp.float32)
    y = x32 * lax.rsqrt(jnp.mean(x32 * x32, axis=-1, keepdims=True) + NORM_EPS)
    return (y * g.astype(jnp.float32)).astype(x.dtype)


def rope(x, pos):
    d = x.shape[-1]
    half = d // 2
    inv = ROPE_THETA ** (-jnp.arange(half, dtype=jnp.float32) * 2.0 / d)
    ang = pos.astype(jnp.float32)[:, None] * inv[None, :]
    cos, sin = jnp.cos(ang), jnp.sin(ang)
    x32 = x.astype(jnp.float32)
    x1, x2 = x32[..., :half], x32[..., half:]
    return jnp.concatenate([x1 * cos - x2 * sin, x1 * sin + x2 * cos], axis=-1).astype(x.dtype)


def causal_attention(q, k, v, scale):
    B, H, S, dq = q.shape
    dv = v.shape[-1]
    nb = S // Q_BLOCK
    qb = q.reshape(B, H, nb, Q_BLOCK, dq).transpose(2, 0, 1, 3, 4)
    kpos = jnp.arange(S)

    def one_block(args):
        q_blk, i = args
        s = jnp.einsum('bhqd,bhkd->bhqk', q_blk, k).astype(jnp.float32) * scale
        qpos = i * Q_BLOCK + jnp.arange(Q_BLOCK)
        s = jnp.where(kpos[None, :] <= qpos[:, None], s, -jnp.inf)
        p = jax.nn.softmax(s, axis=-1)
        return jnp.einsum('bhqk,bhkd->bhqd', p.astype(v.dtype), v)

    out = lax.map(one_block, (qb, jnp.arange(nb)))
    return out.transpose(1, 2, 0, 3, 4).reshape(B, H, S, dv)


def causal_depthwise_conv(x, w, b):
    C = x.shape[-1]
    y = lax.conv_general_dilated(x, w[:, None, :], window_strides=(1,),
                                 padding=((CONV_WIDTH - 1, 0),),
                                 dimension_numbers=('NWC', 'WIO', 'NWC'),
                                 feature_group_count=C)
    return y + b


def mlstm_chunkwise(q, k, v, ig, lf):
    B, H, S, dk = q.shape
    dv = v.shape[-1]
    L = MLSTM_CHUNK
    nc = S // L

    def chunks(t):
        return jnp.moveaxis(t.reshape((B, H, nc, L) + t.shape[3:]), 2, 0)

    xs = (chunks(q), chunks(k), chunks(v), chunks(ig), chunks(lf))
    tril = jnp.tril(jnp.ones((L, L), dtype=bool))

    def step(carry, inp):
        C, n, m_prev = carry
        qc, kc, vc, ic, fc = inp
        b = jnp.cumsum(fc, axis=-1)
        D = b[..., :, None] - b[..., None, :] + ic[..., None, :]
        D = jnp.where(tril, D, -jnp.inf)
        m_inter = b + m_prev[..., None]
        m_row = jnp.maximum(m_inter, jnp.max(D, axis=-1))
        W = jnp.exp(D - m_row[..., None]) * jnp.einsum('bhtd,bhsd->bhts', qc, kc)
        inter = jnp.exp(m_inter - m_row)
        num = jnp.einsum('bhts,bhsv->bhtv', W, vc) + inter[..., None] * jnp.einsum('bhtd,bhdv->bhtv', qc, C)
        den = jnp.sum(W, axis=-1) + inter * jnp.einsum('bhtd,bhd->bht', qc, n)
        h = num / jnp.maximum(jnp.abs(den), jnp.exp(-m_row))[..., None]
        bL = b[..., -1]
        g = bL[..., None] - b + ic
        m_new = jnp.maximum(bL + m_prev, jnp.max(g, axis=-1))
        wk = jnp.exp(g - m_new[..., None])
        decay = jnp.exp(bL + m_prev - m_new)
        C_new = decay[..., None, None] * C + jnp.einsum('bhsd,bhsv->bhdv', kc * wk[..., None], vc)
        n_new = decay[..., None] * n + jnp.einsum('bhs,bhsd->bhd', wk, kc)
        return (C_new, n_new, m_new), h

    init = (jnp.zeros((B, H, dk, dv), jnp.float32), jnp.zeros((B, H, dk), jnp.float32),
            jnp.zeros((B, H), jnp.float32))
    _, h = lax.scan(step, init, xs)
    return jnp.moveaxis(h, 0, 2).reshape(B, H, S, dv)


def even_mixer(h, pos, w_in, q_norm, w_uq, kv_norm, w_ukv, conv_w, conv_b, b_i, b_f, w_out):
    B, S, _ = h.shape
    proj = h @ w_in
    cuts = np.cumsum(EVEN_SPLIT_SIZES)[:-1].tolist()
    c_q, c_kv, k_r, m_qk, m_v, m_i, m_f, m_o = jnp.split(proj, cuts, axis=-1)

    q = (rms_norm(c_q, q_norm) @ w_uq).reshape(B, S, MLA_HEADS, MLA_QK).transpose(0, 2, 1, 3)
    q = jnp.concatenate([q[..., :MLA_NOPE], rope(q[..., MLA_NOPE:], pos)], axis=-1)
    kv = (rms_norm(c_kv, kv_norm) @ w_ukv).reshape(B, S, MLA_HEADS, MLA_NOPE + MLA_V).transpose(0, 2, 1, 3)
    k_nope, v = kv[..., :MLA_NOPE], kv[..., MLA_NOPE:]
    k_rope = rope(k_r, pos)[:, None]
    k = jnp.concatenate([k_nope, jnp.broadcast_to(k_rope, (B, MLA_HEADS, S, MLA_ROPE))], axis=-1)
    a_out = causal_attention(q, k, v, MLA_QK ** -0.5)
    a_out = a_out.transpose(0, 2, 1, 3).reshape(B, S, MLA_HEADS * MLA_V)

    qk = jax.nn.silu(causal_depthwise_conv(m_qk, conv_w, conv_b))
    mq, mk = jnp.split(qk, 2, axis=-1)
    heads = lambda t, e: t.reshape(B, S, MLSTM_HEADS, e).transpose(0, 2, 1, 3).astype(jnp.float32)
    mq = heads(mq, MLSTM_DK)
    mk = heads(mk, MLSTM_DK) * (MLSTM_DK ** -0.5)
    mv = heads(m_v, MLSTM_DV)
    ig = (m_i + b_i).astype(jnp.float32).transpose(0, 2, 1)
    lf = jax.nn.log_sigmoid((m_f + b_f).astype(jnp.float32)).transpose(0, 2, 1)
    hm = mlstm_chunkwise(mq, mk, mv, ig, lf)
    hm = hm.transpose(0, 2, 1, 3).reshape(B, S, MLSTM_HEADS * MLSTM_DV)
    hm = (jax.nn.sigmoid(m_o.astype(jnp.float32)) * hm).astype(h.dtype)

    return jnp.concatenate([a_out, hm], axis=-1) @ w_out


def dilated_branch(q, k, v, window, dilation):
    B, H, S, dh = q.shape
    span = window // dilation
    L = S // dilation
    nb = -(-L // DIL_BLOCK)
    Lp = nb * DIL_BLOCK

    def residues(t):
        t = t.reshape(B, H, L, dilation, dh).transpose(0, 1, 3, 2, 4)
        t = jnp.pad(t, ((0, 0), (0, 0), (0, 0), (0, Lp - L), (0, 0)))
        return t.reshape(B, H, dilation, nb, DIL_BLOCK, dh)

    def band(t):
        prev = jnp.pad(t, ((0, 0), (0, 0), (0, 0), (1, 0), (0, 0), (0, 0)))[:, :, :, :nb]
        return jnp.concatenate([prev, t], axis=4)

    qb = residues(q)
    kw, vw = band(residues(k)), band(residues(v))
    s = jnp.einsum('bhrnqd,bhrnkd->bhrnqk', qb, kw).astype(jnp.float32) * (dh ** -0.5)
    qi = jnp.arange(DIL_BLOCK)[:, None]
    kj = jnp.arange(2 * DIL_BLOCK)[None, :]
    dist = DIL_BLOCK + qi - kj
    key_sub = (jnp.arange(nb)[:, None, None] - 1) * DIL_BLOCK + kj[None]
    mask = (dist >= 0) & (dist <= span) & (key_sub >= 0)
    s = jnp.where(mask, s, -jnp.inf)
    m = jnp.max(s, axis=-1, keepdims=True)
    p = jnp.exp(s - m)
    den = jnp.sum(p, axis=-1)
    o = jnp.einsum('bhrnqk,bhrnkd->bhrnqd', p.astype(vw.dtype), vw).astype(jnp.float32) / den[..., None]
    lse = m[..., 0] + jnp.log(den)

    def merge(t):
        e = t.shape[-1]
        t = t.reshape(B, H, dilation, Lp, e)[:, :, :, :L]
        return t.transpose(0, 1, 3, 2, 4).reshape(B, H, S, e)

    return merge(o), merge(lse[..., None])[..., 0]


def odd_mixer(h, pos, w_qkv, w_out):
    B, S, _ = h.shape
    qkv = (h @ w_qkv).reshape(B, S, 3, DIL_HEADS, DIL_HEAD_DIM).transpose(2, 0, 3, 1, 4)
    q, k, v = rope(qkv[0], pos), rope(qkv[1], pos), qkv[2]
    outs, lses = [], []
    for window, dilation in DIL_PATTERNS:
        o_g, lse_g = dilated_branch(q, k, v, window, dilation)
        outs.append(o_g)
        lses.append(lse_g)
    alpha = jax.nn.softmax(jnp.stack(lses), axis=0)
    o = jnp.einsum('gbhs,gbhsd->bhsd', alpha, jnp.stack(outs))
    o = o.transpose(0, 2, 1, 3).reshape(B, S, ODD_MIX).astype(h.dtype)
    return o @ w_out


def squared_relu_mlp(h, w1, w2):
    return jnp.square(jax.nn.relu(h @ w1)) @ w2


def setup_inputs(seed: int = 0) -> dict:
    key = jax.random.key(seed)
    ks = jax.random.split(key, 18)
    f32 = jnp.float32
    res = (2 * DEPTH) ** -0.5

    def dense(k, shape, fan_in, gain=1.0):
        return jax.random.normal(k, shape, f32) * (gain * fan_in ** -0.5)

    def norm_gain(k, shape):
        return 1.0 + 0.05 * jax.random.normal(k, shape, f32)

    return {
        'x': jax.random.normal(ks[0], (BATCH, SEQ, D_MODEL), f32),
        'norm_mix': norm_gain(ks[1], (DEPTH, D_MODEL)),
        'norm_mlp': norm_gain(ks[2], (DEPTH, D_MODEL)),
        'ev_w_in': dense(ks[3], (N_EVEN, D_MODEL, EVEN_IN), D_MODEL),
        'mla_q_norm': norm_gain(ks[4], (N_EVEN, MLA_Q_LORA)),
        'mla_w_uq': dense(ks[5], (N_EVEN, MLA_Q_LORA, MLA_HEADS * MLA_QK), MLA_Q_LORA),
        'mla_kv_norm': norm_gain(ks[6], (N_EVEN, MLA_KV_LORA)),
        'mla_w_ukv': dense(ks[7], (N_EVEN, MLA_KV_LORA, MLA_HEADS * (MLA_NOPE + MLA_V)), MLA_KV_LORA),
        'mlstm_conv_w': dense(ks[8], (N_EVEN, CONV_WIDTH, 2 * MLSTM_HEADS * MLSTM_DK), CONV_WIDTH),
        'mlstm_conv_b': 0.01 * jax.random.normal(ks[9], (N_EVEN, 2 * MLSTM_HEADS * MLSTM_DK), f32),
        'mlstm_b_i': 0.1 * jax.random.normal(ks[10], (N_EVEN, MLSTM_HEADS), f32),
        'mlstm_b_f': FORGET_BIAS + 0.1 * jax.random.normal(ks[11], (N_EVEN, MLSTM_HEADS), f32),
        'ev_w_out': dense(ks[12], (N_EVEN, EVEN_MIX, D_MODEL), EVEN_MIX, res),
        'od_w_qkv': dense(ks[13], (N_ODD, D_MODEL, 3 * ODD_MIX), D_MODEL),
        'od_w_out': dense(ks[14], (N_ODD, ODD_MIX, D_MODEL), ODD_MIX, res),
        'mlp_w1': dense(ks[15], (DEPTH, D_MODEL, D_FF), D_MODEL),
        'mlp_w2': dense(ks[16], (DEPTH, D_FF, D_MODEL), D_FF, res),
        'norm_final': norm_gain(ks[17], (D_MODEL,)),
    }


def reference(x, norm_mix, norm_mlp, ev_w_in, mla_q_norm, mla_w_uq, mla_kv_norm, mla_w_ukv,
              mlstm_conv_w, mlstm_conv_b, mlstm_b_i, mlstm_b_f, ev_w_out, od_w_qkv, od_w_out,
              mlp_w1, mlp_w2, norm_final):
    pos = jnp.arange(x.shape[1])
    for layer in range(DEPTH):
        i = layer // 2
        h = rms_norm(x, norm_mix[layer])
        if layer % 2 == 0:
            mix = even_mixer(h, pos, ev_w_in[i], mla_q_norm[i], mla_w_uq[i], mla_kv_norm[i],
                             mla_w_ukv[i], mlstm_conv_w[i], mlstm_conv_b[i], mlstm_b_i[i],
                             mlstm_b_f[i], ev_w_out[i])
        else:
            mix = odd_mixer(h, pos, od_w_qkv[i], od_w_out[i])
        x = x + mix
        x = x + squared_relu_mlp(rms_norm(x, norm_mlp[layer]), mlp_w1[layer], mlp_w2[layer])
    return rms_norm(x, norm_final)
```

```python
import contextlib
import math
import numpy as np
import concourse.bass as bass
import concourse.mybir as mybir
from concourse.bass_utils import run_bass_kernel_spmd

F32 = mybir.dt.float32
BF16 = mybir.dt.bfloat16
I32 = mybir.dt.int32
AF = mybir.ActivationFunctionType
ALU = mybir.AluOpType
AX = mybir.AxisListType

S = 2048
D = 2048
KC = D // 128
DFF = 8192
DEPTH = 4
EPS = 1e-6
EVEN_IN = 4168
NCORES = 8


class _Op:
    __slots__ = ("eng", "fn", "reads", "writes", "dma", "grp", "deps", "signal", "sem", "val", "bar")


class Sched:
    NPOOL = {"sp": 24, "pool": 6, "act": 8}

    def __init__(self, nc, es):
        self.nc = nc
        self.es = es
        self.ops = []
        self.engs = {"pe": nc.tensor, "act": nc.scalar, "dve": nc.vector, "pool": nc.gpsimd, "sp": nc.sync}

    def add(self, eng, fn, reads=(), writes=(), dma=False, grp=None, bar=False):
        o = _Op()
        o.eng = eng
        o.fn = fn
        reads = list(reads)
        writes = list(writes)
        for k in list(reads):
            if isinstance(k, str) and k.startswith("PS_"):
                reads.remove(k)
                if k not in writes:
                    writes.append(k)
        o.reads = tuple(reads)
        o.writes = tuple(writes)
        o.dma = dma
        o.grp = grp
        o.bar = bar
        o.deps = None
        o.signal = dma
        o.sem = None
        o.val = None
        self.ops.append(o)

    def pe(self, fn, r=(), w=()):
        self.add("pe", fn, r, w)

    def act(self, fn, r=(), w=()):
        self.add("act", fn, r, w)

    def dve(self, fn, r=(), w=()):
        self.add("dve", fn, r, w)

    def pool(self, fn, r=(), w=()):
        self.add("pool", fn, r, w)

    def dma(self, eng, fn, r, w, grp):
        self.add(eng, fn, r, w, dma=True, grp=grp)

    def barrier(self):
        self.add("sp", lambda: self.nc.sync.nop(), (), (), bar=True)

    def finalize(self):
        ops = self.ops
        nc = self.nc
        last_w = {}
        rd_eng = {}
        rd_dma = {}
        last_on_eng = {}
        last_dma_grp = {}
        dma_rr = {}
        cur_bar = None
        for i, o in enumerate(ops):
            deps = set()
            if o.bar:
                deps.update(last_on_eng.values())
                deps.update(last_dma_grp.values())
                if cur_bar is not None:
                    deps.add(cur_bar)
            else:
                if cur_bar is not None:
                    deps.add(cur_bar)
                for k in o.reads:
                    if k in last_w:
                        deps.add(last_w[k])
                for k in o.writes:
                    if k in last_w:
                        deps.add(last_w[k])
                    if k in rd_eng:
                        deps.update(rd_eng[k].values())
                    if k in rd_dma:
                        deps.update(rd_dma[k])
                for k in o.reads:
                    if o.dma:
                        rd_dma.setdefault(k, []).append(i)
                    else:
                        rd_eng.setdefault(k, {})[o.eng] = i
                for k in o.writes:
                    last_w[k] = i
                    rd_eng.pop(k, None)
                    rd_dma.pop(k, None)
            if o.dma:
                n = dma_rr.get(o.eng, 0)
                dma_rr[o.eng] = n + 1
                o.grp = (o.eng, n % self.NPOOL[o.eng])
                if o.grp in last_dma_grp:
                    deps.add(last_dma_grp[o.grp])
            deps.discard(i)
            if o.eng == "pe" and not o.dma:
                deps = {d for d in deps if not (ops[d].eng == "pe" and not ops[d].dma)}
            o.deps = deps
            if o.bar:
                cur_bar = i
                last_on_eng = {}
            if o.dma:
                last_dma_grp[o.grp] = i
            else:
                last_on_eng[o.eng] = i
        for o in ops:
            for d in o.deps:
                ops[d].signal = True
        eng_sem = {}
        eng_cnt = {}
        grp_sem = {}
        grp_cnt = {}
        for e in self.engs:
            eng_sem[e] = self.es.enter_context(nc.semaphore("sem_" + e))
            eng_cnt[e] = 0
        for o in ops:
            if o.dma:
                if o.grp not in grp_sem:
                    grp_sem[o.grp] = self.es.enter_context(nc.semaphore("semd_%d" % len(grp_sem)))
                    grp_cnt[o.grp] = 0
                grp_cnt[o.grp] += 16
                o.sem = grp_sem[o.grp]
                o.val = grp_cnt[o.grp]
            elif o.signal:
                eng_cnt[o.eng] += 1
                o.sem = eng_sem[o.eng]
                o.val = eng_cnt[o.eng]
        waited = {e: {} for e in self.engs}
        nwait = 0
        for o in ops:
            need = {}
            for d in o.deps:
                od = ops[d]
                key = od.sem.name
                if key not in need or need[key][1] < od.val:
                    need[key] = (od.sem, od.val)
            w = waited[o.eng]
            eng = self.engs[o.eng]
            for key, (sem, val) in need.items():
                if w.get(key, 0) >= val:
                    continue
                eng.wait_ge(sem, val)
                w[key] = val
                nwait += 1
            inst = o.fn()
            if o.signal:
                inst.then_inc(o.sem, 16 if o.dma else 1)
        sp = self.engs["sp"]
        for g, sem in grp_sem.items():
            if waited["sp"].get(sem.name, 0) < grp_cnt[g]:
                sp.wait_ge(sem, grp_cnt[g])
        self.stats = (len(ops), nwait, len(grp_sem))


class Builder:
    def __init__(self, cfg):
        self.cfg = cfg
        self.nc = bass.Bass("TRN2", target_bir_lowering=False)
        self.es = contextlib.ExitStack()
        self.sc = Sched(self.nc, self.es)
        self.uid = 0
        self.inputs = {}

    def sb(self, st, name, shape, dt):
        self.uid += 1
        return st.enter_context(self.nc.sbuf_tensor("%s_%d" % (name, self.uid), list(shape), dt))

    def ps(self, st, name, shape, dt):
        self.uid += 1
        return st.enter_context(self.nc.psum_tensor("PS_%s_%d" % (name, self.uid), list(shape), dt))

    SHAPES = {"x": [S, D], "norm_mix": [DEPTH, D], "norm_mlp": [DEPTH, D], "ev_w_in": [D, EVEN_IN],
              "mla_q_norm": [2, 512], "mla_w_uq": [512, 1536], "mla_kv_norm": [2, 512], "mla_w_ukv": [512, 2048],
              "mlstm_conv_w": [2, 4, 1024], "mlstm_conv_b": [2, 1024], "mlstm_b_i": [2, 4], "mlstm_b_f": [2, 4],
              "ev_w_out": [D, D], "od_w_qkv": [D, 3 * D], "od_w_out": [D, D], "mlp_w1": [D, DFF],
              "mlp_w2": [DFF, D], "norm_final": [D]}
    PER_LAYER = ("ev_w_in", "mla_w_uq", "mla_w_ukv", "ev_w_out", "od_w_qkv", "od_w_out", "mlp_w1", "mlp_w2")

    def inp(self, name, idx=None):
        key = name if idx is None else "%s_%d" % (name, idx)
        if key not in self.inputs:
            self.inputs[key] = self.nc.dram_tensor(key, list(self.SHAPES[name]), F32, kind="ExternalInput").ap()
        return self.inputs[key]

    def build(self):
        nc = self.nc
        sc = self.sc
        cfg = self.cfg
        self.x_d = self.inp("x")
        self.y_d = nc.dram_tensor("y", [S, D], F32, kind="ExternalOutput").ap()
        self.xT_d = nc.dram_tensor("xT_scr", [D, S], F32).ap()
        self.oT_d = nc.dram_tensor("oT_scr", [D, S], BF16).ap()
        self.og_d = nc.dram_tensor("og_scr", [1024, S], BF16).ap()
        self.dbg_d = None
        if cfg.get("dbg"):
            self.dbg_d = nc.dram_tensor("dbg", [D, S], F32, kind="ExternalOutput").ap()

        self.gst = self.es
        self.setup_consts()
        if not cfg.get("skip_load"):
            self.stage_load_x()
        for st in cfg["stages"]:
            kind, l = st
            if kind == "mlp":
                self.stage_mlp(l)
            elif kind == "odd":
                self.stage_odd(l)
            elif kind == "even":
                self.stage_even(l)
        if self.dbg_d is not None:
            self.stage_dbg()
        if not cfg.get("skip_final"):
            self.stage_final()
        sc.finalize()
        return nc

    def setup_consts(self):
        nc, sc = self.nc, self.sc
        g = self.gst
        self.ones_f = self.sb(g, "ones_f", [128, 128], F32)
        self.ones_b = self.sb(g, "ones_b", [128, 128], BF16)
        self.ident_f = self.sb(g, "ident_f", [128, 128], F32)
        self.ident_b = self.sb(g, "ident_b", [128, 128], BF16)
        self.gains = self.sb(g, "gains", [128, 256], F32)
        ones_f, ones_b, ident_f, ident_b = self.ones_f, self.ones_b, self.ident_f, self.ident_b
        sc.pool(lambda: nc.gpsimd.memset(ones_f[:], 1.0), w=["ones_f"])
        sc.pool(lambda: nc.gpsimd.memset(ones_b[:], 1.0), w=["ones_b"])
        sc.pool(lambda: nc.gpsimd.affine_select(ident_f[:], ones_f[:], [[-1, 128]], ALU.is_equal, 0.0,
                                                base=0, channel_multiplier=1), r=["ones_f"], w=["ident_f"])
        sc.pool(lambda: nc.gpsimd.tensor_copy(ident_b[:], ident_f[:]), r=["ident_f"], w=["ident_b"])
        with contextlib.ExitStack() as st:
            p1 = self.sb(st, "p1", [128, 128], F32)
            p2 = self.sb(st, "p2", [128, 128], F32)
            pt = self.ps(st, "pt", [128, 512], F32)
            sc.pool(lambda: nc.gpsimd.memset(p2[:], 0.0), w=["p2"])
            sc.dma("sp", lambda: nc.sync.dma_start(out=p1[0:64, :], in_=self.inp("norm_mix").rearrange("l (c p) -> (l c) p", p=128)),
                   [], ["p1"], "p1")
            sc.dma("sp", lambda: nc.sync.dma_start(out=p1[64:128, :], in_=self.inp("norm_mlp").rearrange("l (c p) -> (l c) p", p=128)),
                   [], ["p1"], "p1")
            sc.dma("sp", lambda: nc.sync.dma_start(out=p2[0:16, :], in_=self.inp("norm_final").rearrange("(c p) -> c p", p=128)),
                   ["p2"], ["p2"], "p2")
            sc.dma("sp", lambda: nc.sync.dma_start(out=p2[16:24, :], in_=self.inp("mla_q_norm").rearrange("l (c p) -> (l c) p", p=128)),
                   ["p2"], ["p2"], "p2")
            sc.dma("sp", lambda: nc.sync.dma_start(out=p2[24:32, :], in_=self.inp("mla_kv_norm").rearrange("l (c p) -> (l c) p", p=128)),
                   ["p2"], ["p2"], "p2")
            sc.dma("sp", lambda: nc.sync.dma_start(out=p2[32:48, :], in_=self.inp("mlstm_conv_b").rearrange("l (c p) -> (l c) p", p=128)),
                   ["p2"], ["p2"], "p2")
            sc.dma("sp", lambda: nc.sync.dma_start(out=p2[48:112, :], in_=self.inp("mlstm_conv_w").rearrange("l j (c p) -> (l j c) p", p=128)),
                   ["p2"], ["p2"], "p2")
            sc.pe(lambda: nc.tensor.transpose(pt[:, 0:128], p1[:], ident_f[:]), r=["p1", "ident_f"], w=["pt"])
            sc.pe(lambda: nc.tensor.transpose(pt[:, 128:256], p2[:], ident_f[:]), r=["p2", "ident_f"], w=["pt"])
            gains = self.gains
            sq = math.sqrt(D)
            sc.dve(lambda: nc.vector.tensor_scalar(gains[:, 0:144], pt[:, 0:144], sq, None, op0=ALU.mult),
                   r=["pt"], w=["gains"])
            sc.dve(lambda: nc.vector.tensor_scalar(gains[:, 144:160], pt[:, 144:160], math.sqrt(512.0), None, op0=ALU.mult),
                   r=["pt"], w=["gains"])
            sc.dve(lambda: nc.vector.tensor_copy(gains[:, 160:240], pt[:, 160:240]), r=["pt"], w=["gains"])
            sc.barrier()

    def gcol(self, kind, l, c):
        if kind == "mix":
            j = l * 16 + c
        elif kind == "mlp":
            j = 64 + l * 16 + c
        elif kind == "final":
            j = 128 + c
        elif kind == "qn":
            j = 144 + l * 4 + c
        elif kind == "kvn":
            j = 152 + l * 4 + c
        elif kind == "convb":
            j = 160 + l * 8 + c
        elif kind == "convw":
            j = 176 + l * 32 + c
        return self.gains[:, j:j + 1]

    def rstd_from(self, out_ap, ps_ap, rkeys, wkeys, eps_total):
        nc, sc = self.nc, self.sc
        sc.dve(lambda: nc.vector.tensor_scalar(out_ap, ps_ap, eps_total, None, op0=ALU.add), r=rkeys, w=wkeys)
        sc.act(lambda: nc.scalar.activation(out=out_ap, in_=out_ap, func=AF.Sqrt), r=wkeys, w=wkeys)
        sc.dve(lambda: nc.vector.reciprocal(out_ap, out_ap), r=wkeys, w=wkeys)

    def stage_load_x(self):
        nc, sc = self.nc, self.sc
        xTv = self.xT_d.rearrange("(c p) t -> p c t", p=128)
        with contextlib.ExitStack() as st:
            xin = [self.sb(st, "xin", [128, D], F32) for _ in range(2)]
            xo = [self.sb(st, "xo", [128, KC, 128], F32) for _ in range(2)]
            pb = [self.ps(st, "pb", [128, 512], F32) for _ in range(4)]
            ident_f = self.ident_f
            for tt in range(16):
                xi = xin[tt % 2]
                xx = xo[tt % 2]
                sc.dma("sp", lambda xi=xi, tt=tt: nc.sync.dma_start(out=xi[:], in_=self.x_d[tt * 128:(tt + 1) * 128, :]),
                       [], [xi.name], xi.name)
                for q in range(4):
                    p = pb[q]
                    for j in range(4):
                        c = q * 4 + j
                        sc.pe(lambda p=p, j=j, c=c, xi=xi: nc.tensor.transpose(p[:, j * 128:(j + 1) * 128], xi[:, c * 128:(c + 1) * 128], ident_f[:]),
                              r=[xi.name, "ident_f"], w=[p.name])
                    if q % 2 == 0:
                        sc.act(lambda p=p, q=q, xx=xx: nc.scalar.copy(xx[:, q * 4:(q + 1) * 4, :], p[:].rearrange("p (a b) -> p a b", b=128)),
                               r=[p.name], w=[xx.name])
                    else:
                        sc.dve(lambda p=p, q=q, xx=xx: nc.vector.tensor_copy(xx[:, q * 4:(q + 1) * 4, :], p[:].rearrange("p (a b) -> p a b", b=128)),
                               r=[p.name], w=[xx.name])
                sc.dma("sp", lambda xx=xx, tt=tt: nc.sync.dma_start(out=xTv[:, :, tt * 128:(tt + 1) * 128], in_=xx[:]),
                       [xx.name], [("xT", c, tt // 4) for c in range(KC)], "xT")
            sc.barrier()

    def norm_block(self, st_tmp, hT, hkeys, tok0, gkind, l, xc, sqb, rstd, pss):
        nc, sc = self.nc, self.sc
        xTv = self.xT_d.rearrange("(c p) t -> p c t", p=128)
        ones_f = self.ones_f
        NT = 1024
        tb0 = tok0 // 512
        hT_t, col0 = hT
        for c in range(KC):
            x_ = xc[c % len(xc)]
            s_ = sqb[c % len(sqb)]
            sc.dma("sp", lambda x_=x_, c=c: nc.sync.dma_start(out=x_[:], in_=xTv[:, c, tok0:tok0 + NT]),
                   [("xT", c, tb0), ("xT", c, tb0 + 1)], [x_.name], x_.name)
            sc.act(lambda x_=x_, s_=s_: nc.scalar.activation(out=s_[:], in_=x_[:], func=AF.Square), r=[x_.name], w=[s_.name])
            for h in range(2):
                sc.pe(lambda h=h, s_=s_, c=c: nc.tensor.matmul(pss[h][:], ones_f[:], s_[:, h * 512:(h + 1) * 512],
                                                              start=(c == 0), stop=(c == KC - 1)),
                      r=[s_.name, "ones_f"], w=[pss[h].name])
        for h in range(2):
            self.rstd_from(rstd[:, h * 512:(h + 1) * 512], pss[h][:], [pss[h].name], [rstd.name], float(D * EPS))
        for c in range(KC):
            x_ = xc[c % len(xc)]
            sc.dma("sp", lambda x_=x_, c=c: nc.sync.dma_start(out=x_[:], in_=xTv[:, c, tok0:tok0 + NT]),
                   [("xT", c, tb0), ("xT", c, tb0 + 1)], [x_.name], x_.name)
            gc = self.gcol(gkind, l, c)
            sc.dve(lambda x_=x_, c=c, gc=gc: nc.vector.scalar_tensor_tensor(
                out=hT_t[:, c, col0:col0 + NT], in0=x_[:], scalar=gc, in1=rstd[:], op0=ALU.mult, op1=ALU.mult),
                r=[x_.name, rstd.name, "gains"], w=hkeys(c))

    def stage_mlp(self, l):
        nc, sc = self.nc, self.sc
        NT = 1024
        xTv = self.xT_d.rearrange("(c p) t -> p c t", p=128)
        w1v = self.inp("mlp_w1", l).rearrange("(kc p) n -> p kc n", p=128)
        w2v = self.inp("mlp_w2", l).rearrange("(m p) n -> p m n", p=128)
        ones_f = self.ones_f
        with contextlib.ExitStack() as st:
            hT = self.sb(st, "hT", [128, KC, NT], BF16)
            uT = self.sb(st, "uT", [128, 32, NT], BF16)
            w1s = [self.sb(st, "w1s", [128, KC, 256], BF16) for _ in range(2)]
            w2s = [self.sb(st, "w2s", [128, 32, 128], BF16) for _ in range(2)]
            xc = [self.sb(st, "xc", [128, NT], F32) for _ in range(3)]
            sqb = [self.sb(st, "sqb", [128, NT], F32) for _ in range(2)]
            rstd = self.sb(st, "rstd", [128, NT], F32)
            rl = [self.sb(st, "rl", [128, 512], F32) for _ in range(2)]
            xr = [self.sb(st, "xr", [128, 512], F32) for _ in range(3)]
            pss = [self.ps(st, "pss", [128, 512], F32) for _ in range(2)]
            pu = [self.ps(st, "pu", [128, 512], F32) for _ in range(3)]
            py = [self.ps(st, "py", [128, 512], F32) for _ in range(3)]
            for blk in range(S // NT):
                tok0 = blk * NT
                self.norm_block(st, (hT, 0), lambda c: [("hT", c)], tok0, "mlp", l, xc, sqb, rstd, pss)
                wi = 0
                w2i = 0
                ui = 0
                yi = 0
                for hh in range(2):
                    for j in range(16):
                        ws = w1s[wi % 2]
                        wi += 1
                        n0 = hh * 4096 + j * 256
                        sc.dma("pool", lambda ws=ws, n0=n0: nc.gpsimd.dma_start(out=ws[:], in_=w1v[:, :, n0:n0 + 256]),
                               [], [ws.name], ws.name)
                        for mm in range(2):
                            m = j * 2 + mm
                            for th in range(2):
                                p = pu[ui % 3]
                                r_ = rl[ui % 2]
                                ui += 1
                                for kc in range(KC):
                                    sc.pe(lambda p=p, ws=ws, mm=mm, kc=kc, th=th: nc.tensor.matmul(
                                        p[:], ws[:, kc, mm * 128:(mm + 1) * 128], hT[:, kc, th * 512:(th + 1) * 512],
                                        start=(kc == 0), stop=(kc == KC - 1)),
                                        r=[ws.name, ("hT", kc)], w=[p.name])
                                sc.act(lambda p=p, r_=r_: nc.scalar.activation(out=r_[:], in_=p[:], func=AF.Relu),
                                       r=[p.name], w=[r_.name])
                                sc.dve(lambda r_=r_, m=m, th=th: nc.vector.tensor_tensor(
                                    uT[:, m, th * 512:(th + 1) * 512], r_[:], r_[:], ALU.mult),
                                    r=[r_.name], w=[("uT", m, th)])
                    for n in range(KC):
                        ws = w2s[w2i % 2]
                        w2i += 1
                        sc.dma("pool", lambda ws=ws, n=n, hh=hh: nc.gpsimd.dma_start(
                            out=ws[:], in_=w2v[:, hh * 32:(hh + 1) * 32, n * 128:(n + 1) * 128]),
                            [], [ws.name], ws.name)
                        for th in range(2):
                            p = py[yi % 3]
                            x_ = xr[yi % 3]
                            yi += 1
                            tb = tok0 // 512 + th
                            sc.dma("sp", lambda x_=x_, n=n, tb=tb: nc.sync.dma_start(out=x_[:], in_=xTv[:, n, tb * 512:(tb + 1) * 512]),
                                   [("xT", n, tb)], [x_.name], x_.name)
                            for m in range(32):
                                sc.pe(lambda p=p, ws=ws, m=m, th=th: nc.tensor.matmul(
                                    p[:], ws[:, m, :], uT[:, m, th * 512:(th + 1) * 512], start=(m == 0), stop=(m == 31)),
                                    r=[ws.name, ("uT", m, th)], w=[p.name])
                            sc.dve(lambda p=p, x_=x_: nc.vector.tensor_tensor(x_[:], x_[:], p[:], ALU.add),
                                   r=[p.name, x_.name], w=[x_.name])
                            sc.dma("sp", lambda x_=x_, n=n, tb=tb: nc.sync.dma_start(out=xTv[:, n, tb * 512:(tb + 1) * 512], in_=x_[:]),
                                   [x_.name], [("xT", n, tb)], "xT")
            sc.barrier()

    def stage_dbg(self):
        nc, sc = self.nc, self.sc
        sc.dma("sp", lambda: nc.sync.dma_start(out=self.dbg_d[:, :], in_=self.xT_d[:, :]),
               [("xT", c, tb) for c in range(KC) for tb in range(4)], ["dbg"], "dbg")
        sc.barrier()

    def stage_final(self):
        nc, sc = self.nc, self.sc
        xTv = self.xT_d.rearrange("(c p) t -> p c t", p=128)
        ones_f, ident_f = self.ones_f, self.ident_f
        with contextlib.ExitStack() as st:
            xa = self.sb(st, "xa", [128, KC, 512], F32)
            sqb = [self.sb(st, "sqf", [128, 512], F32) for _ in range(2)]
            rstd = self.sb(st, "rstdf", [128, 512], F32)
            hn = [self.sb(st, "hn", [128, 512], F32) for _ in range(2)]
            yo = [self.sb(st, "yo", [128, D], F32) for _ in range(4)]
            pss = self.ps(st, "pssf", [128, 512], F32)
            pt = [self.ps(st, "ptf", [128, 512], F32) for _ in range(4)]
            k = 0
            for tb in range(4):
                for c in range(KC):
                    sc.dma("sp", lambda c=c, tb=tb: nc.sync.dma_start(out=xa[:, c, :], in_=xTv[:, c, tb * 512:(tb + 1) * 512]),
                           [("xT", c, tb)], [("xa", c)], "xa")
                    s_ = sqb[c % 2]
                    sc.act(lambda s_=s_, c=c: nc.scalar.activation(out=s_[:], in_=xa[:, c, :], func=AF.Square),
                           r=[("xa", c)], w=[s_.name])
                    sc.pe(lambda s_=s_, c=c: nc.tensor.matmul(pss[:], ones_f[:], s_[:], start=(c == 0), stop=(c == KC - 1)),
                          r=[s_.name, "ones_f"], w=[pss.name])
                if self.cfg.get("fu", 9) < 2:
                    continue
                self.rstd_from(rstd[:], pss[:], [pss.name], [rstd.name], float(D * EPS))
                if self.cfg.get("fu", 9) < 3:
                    continue
                for c in range(KC):
                    h_ = hn[c % 2]
                    gc = self.gcol("final", 0, c)
                    sc.dve(lambda h_=h_, c=c, gc=gc: nc.vector.scalar_tensor_tensor(
                        out=h_[:], in0=xa[:, c, :], scalar=gc, in1=rstd[:], op0=ALU.mult, op1=ALU.mult),
                        r=[("xa", c), rstd.name, "gains"], w=[h_.name])
                    if self.cfg.get("fu", 9) < 4:
                        continue
                    p = pt[k % 4]
                    k += 1
                    for i in range(4):
                        sc.pe(lambda p=p, h_=h_, i=i: nc.tensor.transpose(p[:, i * 128:(i + 1) * 128], h_[:, i * 128:(i + 1) * 128], ident_f[:]),
                              r=[h_.name, "ident_f"], w=[p.name])
                    if self.cfg.get("fu", 9) < 5:
                        continue
                    for i in range(4):
                        y_ = yo[i]
                        if i % 2 == 0:
                            sc.act(lambda p=p, i=i, c=c, y_=y_: nc.scalar.copy(y_[:, c * 128:(c + 1) * 128], p[:, i * 128:(i + 1) * 128]),
                                   r=[p.name], w=[y_.name])
                        else:
                            sc.dve(lambda p=p, i=i, c=c, y_=y_: nc.vector.tensor_copy(y_[:, c * 128:(c + 1) * 128], p[:, i * 128:(i + 1) * 128]),
                                   r=[p.name], w=[y_.name])
                for i in range(4):
                    if self.cfg.get("fu", 9) < 6:
                        continue
                    y_ = yo[i]
                    t0 = tb * 512 + i * 128
                    sc.dma("sp", lambda y_=y_, t0=t0: nc.sync.dma_start(out=self.y_d[t0:t0 + 128, :], in_=y_[:]),
                           [y_.name], [("y", t0)], "y")
            sc.barrier()


_CACHE = {}


def _get_nc(cfg_key, cfg):
    if cfg_key not in _CACHE:
        b = Builder(cfg)
        nc = b.build()
        _CACHE[cfg_key] = (nc, b)
    return _CACHE[cfg_key]


FULL_STAGES = [("even", 0), ("mlp", 0), ("odd", 1), ("mlp", 1), ("even", 2), ("mlp", 2), ("odd", 3), ("mlp", 3)]
INPUT_NAMES = ["norm_mix", "norm_mlp", "ev_w_in", "mla_q_norm", "mla_w_uq", "mla_kv_norm", "mla_w_ukv",
               "mlstm_conv_w", "mlstm_conv_b", "mlstm_b_i", "mlstm_b_f", "ev_w_out", "od_w_qkv", "od_w_out",
               "mlp_w1", "mlp_w2", "norm_final"]


def run(inputs, stages, dbg=False, trace=False, **kw):
    cfg = {"stages": stages, "dbg": dbg}
    cfg.update({k: v for k, v in kw.items() if k != "ncores"})
    nc, b = _get_nc(repr(sorted(cfg.items(), key=str)), cfg)
    x = np.ascontiguousarray(inputs["x"], dtype=np.float32)
    shared = {}
    for k in b.inputs:
        if k == "x":
            continue
        base, _, idx = k.rpartition("_")
        if base in Builder.PER_LAYER:
            shared[k] = np.ascontiguousarray(inputs[base][int(idx)], dtype=np.float32)
        else:
            shared[k] = np.ascontiguousarray(inputs[k], dtype=np.float32)
    ncores = kw.get("ncores", NCORES)
    in_maps = []
    for c in range(ncores):
        m = dict(shared)
        m["x"] = np.ascontiguousarray(x[c % 4])
        in_maps.append(m)
    res = run_bass_kernel_spmd(nc, in_maps, core_ids=list(range(ncores)), trace=trace)
    return res


def kernel(**inputs):
    res = run(inputs, FULL_STAGES)
    out = np.stack([np.asarray(res.results[b]["y"], dtype=np.float32) for b in range(4)], axis=0)
    return out


def _rope_tables(self, st, d):
    nc, sc = self.nc, self.sc
    half = d // 2
    cos = self.sb(st, "cos", [128, S], F32)
    sin = self.sb(st, "sin", [128, S], F32)
    with contextlib.ExitStack() as t:
        pid = self.sb(t, "pid", [128, 1], I32)
        pim = self.sb(t, "pim", [128, 1], I32)
        pf = self.sb(t, "pf", [128, 1], F32)
        inv = self.sb(t, "inv", [128, 1], F32)
        sgn = self.sb(t, "sgn", [128, 1], F32)
        tpos = self.sb(t, "tpos", [128, S], F32)
        u = self.sb(t, "u", [128, S], F32)
        v = self.sb(t, "v", [128, S], F32)
        vi = self.sb(t, "vi", [128, S], I32)
        fr = self.sb(t, "fr", [128, S], F32)
        sc.pool(lambda: nc.gpsimd.iota(pid[:], [[0, 1]], base=0, channel_multiplier=1), w=[pid.name])
        sc.dve(lambda: nc.vector.tensor_single_scalar(pim[:], pid[:], half - 1, ALU.bitwise_and), r=[pid.name], w=[pim.name])
        sc.dve(lambda: nc.vector.tensor_copy(pf[:], pim[:]), r=[pim.name], w=[pf.name])
        sc.act(lambda: nc.scalar.activation(out=inv[:], in_=pf[:], func=AF.Exp, scale=-(2.0 * math.log(10000.0) / d)),
               r=[pf.name], w=[inv.name])
        sc.dve(lambda: nc.vector.tensor_single_scalar(pim[:], pid[:], half, ALU.bitwise_and), r=[pid.name, pf.name], w=[pim.name])
        sc.dve(lambda: nc.vector.tensor_copy(sgn[:], pim[:]), r=[pim.name], w=[sgn.name])
        sc.dve(lambda: nc.vector.tensor_scalar(sgn[:], sgn[:], 2.0 / half, -1.0, op0=ALU.mult, op1=ALU.add),
               r=[sgn.name], w=[sgn.name])
        sc.pool(lambda: nc.gpsimd.iota(tpos[:], [[1, S]], base=0, channel_multiplier=0, allow_small_or_imprecise_dtypes=True),
                w=[tpos.name])
        sc.dve(lambda: nc.vector.tensor_scalar(u[:], tpos[:], inv[:, 0:1], 1.0 / (2.0 * math.pi), op0=ALU.mult, op1=ALU.mult),
               r=[tpos.name, inv.name], w=[u.name])
        for dst, shift in ((sin, 0.0), (cos, 0.25)):
            sc.dve(lambda shift=shift: nc.vector.tensor_scalar_add(v[:], u[:], shift), r=[u.name], w=[v.name])
            sc.dve(lambda: nc.vector.tensor_copy(vi[:], v[:]), r=[v.name], w=[vi.name])
            sc.dve(lambda: nc.vector.tensor_copy(fr[:], vi[:]), r=[vi.name], w=[fr.name])
            sc.dve(lambda: nc.vector.tensor_sub(fr[:], v[:], fr[:]), r=[v.name, fr.name], w=[fr.name])
            sc.dve(lambda: nc.vector.tensor_single_scalar(v[:], fr[:], 0.5, ALU.is_gt), r=[fr.name], w=[v.name])
            sc.dve(lambda: nc.vector.tensor_sub(fr[:], fr[:], v[:]), r=[v.name, fr.name], w=[fr.name])
            sc.dve(lambda: nc.vector.tensor_single_scalar(v[:], fr[:], -0.5, ALU.is_lt), r=[fr.name], w=[v.name])
            sc.dve(lambda: nc.vector.tensor_add(fr[:], fr[:], v[:]), r=[v.name, fr.name], w=[fr.name])
            sc.act(lambda dst=dst: nc.scalar.activation(out=dst[:], in_=fr[:], func=AF.Sin, scale=2.0 * math.pi * (1.0 - 2e-7)),
                   r=[fr.name], w=[dst.name])
        sc.dve(lambda: nc.vector.tensor_scalar(sin[:], sin[:], sgn[:, 0:1], None, op0=ALU.mult), r=[sin.name, sgn.name], w=[sin.name])
        sc.barrier()
    return cos, sin


def _rope_apply(self, ps_ap, np_, half, cos_ap, sin_ap, out_ap, tmp, rkeys, wkeys, k):
    nc, sc = self.nc, self.sc
    xsw, t1 = tmp
    n = np_
    sc.act(lambda: nc.scalar.copy(xsw[0:half, :], ps_ap[half:n, :]), r=rkeys, w=[xsw.name])
    sc.act(lambda: nc.scalar.copy(xsw[half:n, :], ps_ap[0:half, :]), r=rkeys, w=[xsw.name])
    sc.dve(lambda: nc.vector.tensor_tensor(t1[0:n, :], ps_ap, cos_ap, ALU.mult), r=rkeys + ["ropetab"], w=[t1.name])
    sc.pool(lambda: nc.gpsimd.tensor_tensor(xsw[0:n, :], xsw[0:n, :], sin_ap, ALU.mult), r=[xsw.name, "ropetab"], w=[xsw.name])
    if k % 2 == 0:
        sc.pool(lambda: nc.gpsimd.tensor_tensor(out_ap, t1[0:n, :], xsw[0:n, :], ALU.add), r=[xsw.name, t1.name], w=wkeys)
    else:
        sc.dve(lambda: nc.vector.tensor_tensor(out_ap, t1[0:n, :], xsw[0:n, :], ALU.add), r=[xsw.name, t1.name], w=wkeys)


def _dil_masks(self, st):
    nc, sc = self.nc, self.sc
    d0s = [-384, -256, -128, 0, 128, 256, 384, 512, 640]
    masks = {d0: self.sb(st, "mask", [128, 512], BF16) for d0 in d0s}
    with contextlib.ExitStack() as t:
        di = self.sb(t, "di", [128, 512], I32)
        da = self.sb(t, "da", [128, 512], I32)
        c0 = self.sb(t, "c0", [128, 512], F32)
        c1 = self.sb(t, "c1", [128, 512], F32)
        c2 = self.sb(t, "c2", [128, 512], F32)
        for d0 in d0s:
            m = masks[d0]
            sc.pool(lambda d0=d0: nc.gpsimd.iota(di[:], [[1, 512]], base=d0, channel_multiplier=-1), r=[c0.name], w=[di.name])
            sc.dve(lambda: nc.vector.tensor_single_scalar(c1[:], di[:], 128, ALU.is_le), r=[di.name], w=[c1.name])
            sc.dve(lambda: nc.vector.tensor_single_scalar(da[:], di[:], 3, ALU.bitwise_and), r=[di.name], w=[da.name])
            sc.dve(lambda: nc.vector.tensor_single_scalar(c2[:], da[:], 0, ALU.is_equal), r=[da.name], w=[c2.name])
            sc.dve(lambda: nc.vector.tensor_single_scalar(c0[:], di[:], 512, ALU.is_le), r=[di.name], w=[c0.name])
            sc.dve(lambda: nc.vector.tensor_tensor(c2[:], c2[:], c0[:], ALU.mult), r=[c0.name, c2.name], w=[c2.name])
            sc.dve(lambda: nc.vector.tensor_tensor(c1[:], c1[:], c2[:], ALU.add), r=[c1.name, c2.name], w=[c1.name])
            sc.dve(lambda: nc.vector.tensor_single_scalar(da[:], di[:], 15, ALU.bitwise_and), r=[di.name], w=[da.name])
            sc.dve(lambda: nc.vector.tensor_single_scalar(c2[:], da[:], 0, ALU.is_equal), r=[da.name], w=[c2.name])
            sc.dve(lambda: nc.vector.tensor_tensor(c1[:], c1[:], c2[:], ALU.add), r=[c1.name, c2.name], w=[c1.name])
            sc.dve(lambda: nc.vector.tensor_single_scalar(c0[:], di[:], 0, ALU.is_ge), r=[di.name], w=[c0.name])
            sc.dve(lambda m=m: nc.vector.tensor_tensor(m[:], c1[:], c0[:], ALU.mult), r=[c0.name, c1.name], w=["masks"])
        sc.barrier()
    return masks


def _out_proj(self, w_ap, l):
    nc, sc = self.nc, self.sc
    xTv = self.xT_d.rearrange("(c p) t -> p c t", p=128)
    oTv = self.oT_d.rearrange("(c p) t -> p c t", p=128)
    wv = w_ap.rearrange("(kc p) n -> p kc n", p=128)
    with contextlib.ExitStack() as st:
        oT = self.sb(st, "oT", [128, KC, S], BF16)
        ws_ = [self.sb(st, "wo", [128, KC, 256], BF16) for _ in range(2)]
        xr = [self.sb(st, "xr", [128, 512], F32) for _ in range(3)]
        py = [self.ps(st, "py", [128, 512], F32) for _ in range(3)]
        for c in range(KC):
            sc.dma("sp", lambda c=c: nc.sync.dma_start(out=oT[:, c, :], in_=oTv[:, c, :]), [("oT_d", c)], [("oT", c)], None)
        yi = 0
        for j in range(8):
            ws = ws_[j % 2]
            sc.dma("pool", lambda ws=ws, j=j: nc.gpsimd.dma_start(out=ws[:], in_=wv[:, :, j * 256:(j + 1) * 256]), [], [ws.name], None)
            for nn in range(2):
                n = j * 2 + nn
                for tb in range(4):
                    p = py[yi % 3]
                    x_ = xr[yi % 3]
                    yi += 1
                    sc.dma("sp", lambda x_=x_, n=n, tb=tb: nc.sync.dma_start(out=x_[:], in_=xTv[:, n, tb * 512:(tb + 1) * 512]),
                           [("xT", n, tb)], [x_.name], None)
                    for kc in range(KC):
                        sc.pe(lambda p=p, ws=ws, nn=nn, kc=kc, tb=tb: nc.tensor.matmul(
                            p[:], ws[:, kc, nn * 128:(nn + 1) * 128], oT[:, kc, tb * 512:(tb + 1) * 512],
                            start=(kc == 0), stop=(kc == KC - 1)), r=[ws.name, ("oT", kc)], w=[p.name])
                    sc.dve(lambda p=p, x_=x_: nc.vector.tensor_tensor(x_[:], x_[:], p[:], ALU.add), r=[p.name, x_.name], w=[x_.name])
                    sc.dma("sp", lambda x_=x_, n=n, tb=tb: nc.sync.dma_start(out=xTv[:, n, tb * 512:(tb + 1) * 512], in_=x_[:]),
                           [x_.name], [("xT", n, tb)], None)
        sc.barrier()


def _stage_odd(self, l):
    nc, sc = self.nc, self.sc
    i = l // 2
    wqkv = self.inp("od_w_qkv", i).rearrange("(kc p) n -> p kc n", p=128)
    oTv = self.oT_d.rearrange("(c p) t -> p c t", p=128)
    scale = 128.0 ** -0.5
    ones_b = self.ones_b
    with contextlib.ExitStack() as st:
        cos, sin = self.rope_tables(st, 128)
        masks = self.dil_masks(st)
        hT = self.sb(st, "hT", [128, KC, S], BF16)
        with contextlib.ExitStack() as t:
            xc = [self.sb(t, "xc", [128, 1024], F32) for _ in range(3)]
            sqb = [self.sb(t, "sqb", [128, 1024], F32) for _ in range(2)]
            rstd = self.sb(t, "rstd", [128, 1024], F32)
            pss = [self.ps(t, "pss", [128, 512], F32) for _ in range(2)]
            for blk in range(2):
                self.norm_block(t, (hT, blk * 1024), lambda c, blk=blk: [("hT", c, blk)], blk * 1024, "mix", l, xc, sqb, rstd, pss)
            sc.barrier()
        QT = self.sb(st, "QT", [128, 2, S], BF16)
        KT = self.sb(st, "KT", [128, 2, S], BF16)
        V = self.sb(st, "V", [128, 16, 256], BF16)
        wsl = [self.sb(st, "wq", [128, KC, 256], BF16) for _ in range(3)]
        xsw = [self.sb(st, "xsw", [128, 512], F32) for _ in range(2)]
        t1 = [self.sb(st, "t1", [128, 512], F32) for _ in range(2)]
        E = [self.sb(st, "E", [128, 512], BF16) for _ in range(3)]
        rden = self.sb(st, "rden", [128, 512], F32)
        ot = [self.sb(st, "ot", [128, 512], BF16) for _ in range(2)]
        pq = [self.ps(st, "pq", [128, 512], F32) for _ in range(2)]
        pv = self.ps(st, "pv", [128, 512], F32)
        pS = [self.ps(st, "pS", [128, 512], F32) for _ in range(2)]
        po = self.ps(st, "po", [128, 512], F32)
        pd = self.ps(st, "pd", [128, 512], F32)
        hkeys_all = lambda kc: [("hT", kc, 0), ("hT", kc, 1)]
        wi = 0
        qi = 0
        ei = 0
        oi = 0
        for g in range(8):
            wts = []
            for part in range(3):
                ws = wsl[wi % 3]
                wi += 1
                c0 = part * 2048 + g * 256
                sc.dma("pool", lambda ws=ws, c0=c0: nc.gpsimd.dma_start(out=ws[:], in_=wqkv[:, :, c0:c0 + 256]), [], [ws.name], None)
                wts.append(ws)
            for part, dst, dname in ((0, QT, "QT"), (1, KT, "KT")):
                ws = wts[part]
                for hh in range(2):
                    for tb in range(4):
                        p = pq[qi % 2]
                        tmp = (xsw[qi % 2], t1[qi % 2])
                        for kc in range(KC):
                            sc.pe(lambda p=p, ws=ws, hh=hh, kc=kc, tb=tb: nc.tensor.matmul(
                                p[:], ws[:, kc, hh * 128:(hh + 1) * 128], hT[:, kc, tb * 512:(tb + 1) * 512],
                                start=(kc == 0), stop=(kc == KC - 1)), r=[ws.name, ("hT", kc, tb // 2)], w=[p.name])
                        self.rope_apply(p[:], 128, 64, cos[:, tb * 512:(tb + 1) * 512], sin[:, tb * 512:(tb + 1) * 512],
                                        dst[:, hh, tb * 512:(tb + 1) * 512], tmp, [p.name], [(dname, hh, tb)], qi)
                        qi += 1
            ws = wts[2]
            for tt in range(16):
                for kc in range(KC):
                    sc.pe(lambda ws=ws, kc=kc, tt=tt: nc.tensor.matmul(
                        pv[:, 0:256], hT[:, kc, tt * 128:(tt + 1) * 128], ws[:, kc, :], start=(kc == 0), stop=(kc == KC - 1)),
                        r=[ws.name, ("hT", kc, tt // 8)], w=[pv.name])
                sc.act(lambda tt=tt: nc.scalar.copy(V[:, tt, :], pv[:, 0:256]), r=[pv.name], w=[("V", tt)])
            for hh in range(2):
                head = g * 2 + hh
                for Qs in range(4):
                    nkb = 4 * Qs + 4
                    for kb in range(nkb):
                        d0 = 512 * Qs - 128 * kb
                        m = masks[min(d0, 640)]
                        p = pS[ei % 2]
                        e_ = E[ei % 3]
                        sc.pe(lambda p=p, hh=hh, kb=kb, Qs=Qs: nc.tensor.matmul(
                            p[:], KT[:, hh, kb * 128:(kb + 1) * 128], QT[:, hh, Qs * 512:(Qs + 1) * 512], start=True, stop=True),
                            r=[("KT", hh, kb // 4), ("QT", hh, Qs)], w=[p.name])
                        sc.act(lambda p=p, e_=e_: nc.scalar.activation(out=e_[:], in_=p[:], func=AF.Exp, scale=scale),
                               r=[p.name], w=[e_.name])
                        if ei % 2 == 0:
                            sc.pool(lambda e_=e_, m=m: nc.gpsimd.tensor_tensor(e_[:], e_[:], m[:], ALU.mult), r=[e_.name, "masks"], w=[e_.name])
                        else:
                            sc.dve(lambda e_=e_, m=m: nc.vector.tensor_tensor(e_[:], e_[:], m[:], ALU.mult), r=[e_.name, "masks"], w=[e_.name])
                        ei += 1
                        sc.pe(lambda e_=e_, hh=hh, kb=kb, nkb=nkb: nc.tensor.matmul(
                            po[:], V[:, kb, hh * 128:(hh + 1) * 128], e_[:], start=(kb == 0), stop=(kb == nkb - 1)),
                            r=[e_.name, ("V", kb)], w=[po.name])
                        sc.pe(lambda e_=e_, kb=kb, nkb=nkb: nc.tensor.matmul(
                            pd[:], ones_b[:], e_[:], start=(kb == 0), stop=(kb == nkb - 1)),
                            r=[e_.name, "ones_b"], w=[pd.name])
                    o_ = ot[oi % 2]
                    oi += 1
                    sc.dve(lambda: nc.vector.reciprocal(rden[:], pd[:]), r=[pd.name], w=[rden.name])
                    sc.dve(lambda o_=o_: nc.vector.tensor_tensor(o_[:], po[:], rden[:], ALU.mult), r=[po.name, rden.name], w=[o_.name])
                    sc.dma("sp", lambda o_=o_, head=head, Qs=Qs: nc.sync.dma_start(out=oTv[:, head, Qs * 512:(Qs + 1) * 512], in_=o_[:]),
                           [o_.name], [("oT_d", head)], None)
        sc.barrier()
    self.out_proj(self.inp("od_w_out", i), l)


Builder.rope_tables = _rope_tables
Builder.rope_apply = _rope_apply
Builder.dil_masks = _dil_masks
Builder.out_proj = _out_proj
Builder.stage_odd = _stage_odd


def _norm_tmp(self, t):
    xc = [self.sb(t, "xc", [128, 1024], F32) for _ in range(3)]
    sqb = [self.sb(t, "sqb", [128, 1024], F32) for _ in range(2)]
    rstd = self.sb(t, "rstd", [128, 1024], F32)
    pss = [self.ps(t, "pss", [128, 512], F32) for _ in range(2)]
    return xc, sqb, rstd, pss


def _mla(self, l):
    nc, sc = self.nc, self.sc
    i = l // 2
    w_in = self.inp("ev_w_in", i).rearrange("(kc p) n -> p kc n", p=128)
    wuq_d = self.inp("mla_w_uq", i).rearrange("(c p) n -> p c n", p=128)
    wukv_d = self.inp("mla_w_ukv", i).rearrange("(c p) n -> p c n", p=128)
    oTv = self.oT_d.rearrange("(c p) t -> p c t", p=128)
    ones_f, ones_b = self.ones_f, self.ones_b
    scale = 192.0 ** -0.5
    with contextlib.ExitStack() as st:
        cos, sin = self.rope_tables(st, 64)
        cqn = self.sb(st, "cqn", [128, 4, S], BF16)
        ckvn = self.sb(st, "ckvn", [128, 4, S], BF16)
        krope = self.sb(st, "krope", [128, S], BF16)
        with contextlib.ExitStack() as t:
            hT = self.sb(t, "hT", [128, KC, S], BF16)
            wsl = [self.sb(t, "wl", [128, KC, 256], BF16) for _ in range(2)]
            lat = self.sb(t, "lat", [128, 4, 512], F32)
            sq = [self.sb(t, "sql", [128, 512], F32) for _ in range(2)]
            rs = self.sb(t, "rsl", [128, 512], F32)
            xsw = [self.sb(t, "xsw", [128, 512], F32) for _ in range(2)]
            t1 = [self.sb(t, "t1", [128, 512], F32) for _ in range(2)]
            pl = [self.ps(t, "pl", [128, 512], F32) for _ in range(3)]
            pn = self.ps(t, "pn", [128, 512], F32)
            with contextlib.ExitStack() as t2:
                xc, sqb, rstd, pss = self.norm_tmp(t2)
                for blk in range(2):
                    self.norm_block(t2, (hT, blk * 1024), lambda c, blk=blk: [("hT", c, blk)], blk * 1024, "mix", l, xc, sqb, rstd, pss)
                sc.barrier()
            wi = 0
            pi = 0
            for which, dst, gk in ((0, cqn, "qn"), (1, ckvn, "kvn")):
                wts = []
                for half in range(2):
                    ws = wsl[wi % 2]
                    wi += 1
                    c0 = which * 512 + half * 256
                    sc.dma("pool", lambda ws=ws, c0=c0: nc.gpsimd.dma_start(out=ws[:], in_=w_in[:, :, c0:c0 + 256]), [], [ws.name], None)
                    wts.append(ws)
                for tb in range(4):
                    for c in range(4):
                        ws = wts[c // 2]
                        p = pl[pi % 3]
                        pi += 1
                        for kc in range(KC):
                            sc.pe(lambda p=p, ws=ws, c=c, kc=kc, tb=tb: nc.tensor.matmul(
                                p[:], ws[:, kc, (c % 2) * 128:(c % 2 + 1) * 128], hT[:, kc, tb * 512:(tb + 1) * 512],
                                start=(kc == 0), stop=(kc == KC - 1)), r=[ws.name, ("hT", kc, tb // 2)], w=[p.name])
                        s_ = sq[c % 2]
                        sc.act(lambda p=p, c=c: nc.scalar.copy(lat[:, c, :], p[:]), r=[p.name], w=[("lat", c)])
                        sc.act(lambda s_=s_, c=c: nc.scalar.activation(out=s_[:], in_=lat[:, c, :], func=AF.Square), r=[("lat", c)], w=[s_.name])
                        sc.pe(lambda s_=s_, c=c: nc.tensor.matmul(pn[:], ones_f[:], s_[:], start=(c == 0), stop=(c == 3)),
                              r=[s_.name, "ones_f"], w=[pn.name])
                    self.rstd_from(rs[:], pn[:], [pn.name], [rs.name], float(512 * EPS))
                    for c in range(4):
                        gc = self.gcol(gk, i, c)
                        sc.dve(lambda c=c, gc=gc, dst=dst, tb=tb: nc.vector.scalar_tensor_tensor(
                            out=dst[:, c, tb * 512:(tb + 1) * 512], in0=lat[:, c, :], scalar=gc, in1=rs[:], op0=ALU.mult, op1=ALU.mult),
                            r=[("lat", c), rs.name, "gains"], w=[(dst.name, tb)])
            ws = wsl[wi % 2]
            wi += 1
            sc.dma("pool", lambda ws=ws: nc.gpsimd.dma_start(out=ws[:, :, 0:64], in_=w_in[:, :, 1024:1088]), [], [ws.name], None)
            for tb in range(4):
                p = pl[pi % 3]
                pi += 1
                for kc in range(KC):
                    sc.pe(lambda p=p, ws=ws, kc=kc, tb=tb: nc.tensor.matmul(
                        p[0:64, :], ws[:, kc, 0:64], hT[:, kc, tb * 512:(tb + 1) * 512], start=(kc == 0), stop=(kc == KC - 1)),
                        r=[ws.name, ("hT", kc, tb // 2)], w=[p.name])
                self.rope_apply(p[0:64, :], 64, 32, cos[0:64, tb * 512:(tb + 1) * 512], sin[0:64, tb * 512:(tb + 1) * 512],
                                krope[0:64, tb * 512:(tb + 1) * 512], (xsw[tb % 2], t1[tb % 2]), [p.name], [("krope", tb)], tb)
            sc.barrier()
        wuq = self.sb(st, "wuq", [128, 4, 1536], BF16)
        wukv = self.sb(st, "wukv", [128, 4, 2048], BF16)
        sc.dma("pool", lambda: nc.gpsimd.dma_start(out=wuq[:], in_=wuq_d), [], [wuq.name], None)
        sc.dma("pool", lambda: nc.gpsimd.dma_start(out=wukv[:], in_=wukv_d), [], [wukv.name], None)
        QN = self.sb(st, "QN", [128, S], BF16)
        QR = self.sb(st, "QR", [128, S], BF16)
        KN = self.sb(st, "KN", [128, S], BF16)
        VM = self.sb(st, "VM", [128, 16, 128], BF16)
        cm = {d0: self.sb(st, "cm", [128, 512], BF16) for d0 in (0, -128, -256, -384)}
        xsw = [self.sb(st, "xsw", [128, 512], F32) for _ in range(2)]
        t1 = [self.sb(st, "t1", [128, 512], F32) for _ in range(2)]
        E = [self.sb(st, "E", [128, 512], BF16) for _ in range(3)]
        rden = self.sb(st, "rden", [128, 512], F32)
        ot = [self.sb(st, "ot", [128, 512], BF16) for _ in range(2)]
        di = self.sb(st, "di", [128, 512], I32)
        pq = [self.ps(st, "pq", [128, 512], F32) for _ in range(2)]
        pv = self.ps(st, "pv", [128, 512], F32)
        pS = [self.ps(st, "pS", [128, 512], F32) for _ in range(2)]
        po = self.ps(st, "po", [128, 512], F32)
        pd = self.ps(st, "pd", [128, 512], F32)
        for d0, m in cm.items():
            sc.pool(lambda d0=d0: nc.gpsimd.iota(di[:], [[1, 512]], base=d0, channel_multiplier=-1), r=["cmtmp"], w=[di.name])
            sc.dve(lambda m=m: nc.vector.tensor_single_scalar(m[:], di[:], 0, ALU.is_ge), r=[di.name], w=["cm", "cmtmp"])
        qi = 0
        ei = 0
        oi = 0
        for h in range(8):
            for tb in range(4):
                for kind in range(3):
                    p = pq[qi % 2]
                    if kind == 0:
                        wt, c0, np_ = wuq, h * 192, 128
                        src, sname = cqn, cqn.name
                    elif kind == 1:
                        wt, c0, np_ = wukv, h * 256, 128
                        src, sname = ckvn, ckvn.name
                    else:
                        wt, c0, np_ = wuq, h * 192 + 128, 64
                        src, sname = cqn, cqn.name
                    for c in range(4):
                        sc.pe(lambda p=p, wt=wt, c0=c0, np_=np_, c=c, tb=tb, src=src: nc.tensor.matmul(
                            p[0:np_, :], wt[:, c, c0:c0 + np_], src[:, c, tb * 512:(tb + 1) * 512], start=(c == 0), stop=(c == 3)),
                            r=[wt.name, (sname, tb)], w=[p.name])
                    if kind == 0:
                        sc.act(lambda p=p, tb=tb: nc.scalar.copy(QN[:, tb * 512:(tb + 1) * 512], p[:]), r=[p.name], w=[("QN", tb)])
                    elif kind == 1:
                        sc.act(lambda p=p, tb=tb: nc.scalar.copy(KN[:, tb * 512:(tb + 1) * 512], p[:]), r=[p.name], w=[("KN", tb)])
                    else:
                        self.rope_apply(p[0:64, :], 64, 32, cos[0:64, tb * 512:(tb + 1) * 512], sin[0:64, tb * 512:(tb + 1) * 512],
                                        QR[0:64, tb * 512:(tb + 1) * 512], (xsw[qi % 2], t1[qi % 2]), [p.name], [("QR", tb)], qi)
                    qi += 1
            for tt in range(16):
                for c in range(4):
                    sc.pe(lambda c=c, tt=tt, h=h: nc.tensor.matmul(
                        pv[:, 0:128], ckvn[:, c, tt * 128:(tt + 1) * 128], wukv[:, c, h * 256 + 128:h * 256 + 256],
                        start=(c == 0), stop=(c == 3)), r=[wukv.name, (ckvn.name, tt // 4)], w=[pv.name])
                sc.act(lambda tt=tt: nc.scalar.copy(VM[:, tt, :], pv[:, 0:128]), r=[pv.name], w=[("VM", tt)])
            for Qs in range(4):
                nkb = 4 * Qs + 4
                for kb in range(nkb):
                    d0 = 512 * Qs - 128 * kb
                    p = pS[ei % 2]
                    e_ = E[ei % 3]
                    sc.pe(lambda p=p, kb=kb, Qs=Qs: nc.tensor.matmul(
                        p[:], KN[:, kb * 128:(kb + 1) * 128], QN[:, Qs * 512:(Qs + 1) * 512], start=True, stop=False),
                        r=[("KN", kb // 4), ("QN", Qs)], w=[p.name])
                    sc.pe(lambda p=p, kb=kb, Qs=Qs: nc.tensor.matmul(
                        p[:], krope[0:64, kb * 128:(kb + 1) * 128], QR[0:64, Qs * 512:(Qs + 1) * 512], start=False, stop=True),
                        r=[("krope", kb // 4), ("QR", Qs)], w=[p.name])
                    sc.act(lambda p=p, e_=e_: nc.scalar.activation(out=e_[:], in_=p[:], func=AF.Exp, scale=scale), r=[p.name], w=[e_.name])
                    if d0 <= 0:
                        m = cm[d0]
                        if ei % 2 == 0:
                            sc.pool(lambda e_=e_, m=m: nc.gpsimd.tensor_tensor(e_[:], e_[:], m[:], ALU.mult), r=[e_.name, "cm"], w=[e_.name])
                        else:
                            sc.dve(lambda e_=e_, m=m: nc.vector.tensor_tensor(e_[:], e_[:], m[:], ALU.mult), r=[e_.name, "cm"], w=[e_.name])
                    ei += 1
                    sc.pe(lambda e_=e_, kb=kb, nkb=nkb: nc.tensor.matmul(
                        po[:], VM[:, kb, :], e_[:], start=(kb == 0), stop=(kb == nkb - 1)), r=[e_.name, ("VM", kb)], w=[po.name])
                    sc.pe(lambda e_=e_, kb=kb, nkb=nkb: nc.tensor.matmul(
                        pd[:], ones_b[:], e_[:], start=(kb == 0), stop=(kb == nkb - 1)), r=[e_.name, "ones_b"], w=[pd.name])
                o_ = ot[oi % 2]
                oi += 1
                sc.dve(lambda: nc.vector.reciprocal(rden[:], pd[:]), r=[pd.name], w=[rden.name])
                sc.dve(lambda o_=o_: nc.vector.tensor_tensor(o_[:], po[:], rden[:], ALU.mult), r=[po.name, rden.name], w=[o_.name])
                sc.dma("sp", lambda o_=o_, h=h, Qs=Qs: nc.sync.dma_start(out=oTv[:, h, Qs * 512:(Qs + 1) * 512], in_=o_[:]),
                       [o_.name], [("oT_d", h)], None)
        sc.barrier()


Builder.norm_tmp = _norm_tmp
Builder.mla = _mla


def _mlstm(self, l):
    nc, sc = self.nc, self.sc
    i = l // 2
    w_in = self.inp("ev_w_in", i).rearrange("(kc p) n -> p kc n", p=128)
    oTv = self.oT_d.rearrange("(c p) t -> p c t", p=128)
    ogv = self.og_d.rearrange("(c p) t -> p c t", p=128)
    ones_f, ones_b, ident_f, ident_b = self.ones_f, self.ones_b, self.ident_f, self.ident_b
    NCH = 16
    with contextlib.ExitStack() as st:
        MQ = self.sb(st, "MQ", [128, 4, S], BF16)
        MK = self.sb(st, "MK", [128, 4, S], BF16)
        VL = self.sb(st, "VL", [128, 16, 1024], BF16)
        G1 = self.sb(st, "G1", [4, S], F32)
        G2 = self.sb(st, "G2", [4, S], F32)
        G3 = self.sb(st, "G3", [4, S], F32)
        bi = self.sb(st, "bi", [4, 1], F32)
        nbf = self.sb(st, "nbf", [4, 1], F32)
        sc.dma("sp", lambda: nc.sync.dma_start(out=bi[:], in_=self.inp("mlstm_b_i")[i].rearrange("(h o) -> h o", o=1)), [], [bi.name], None)
        sc.dma("sp", lambda: nc.sync.dma_start(out=nbf[:], in_=self.inp("mlstm_b_f")[i].rearrange("(h o) -> h o", o=1)), [], [nbf.name], None)
        sc.dve(lambda: nc.vector.tensor_scalar(nbf[:], nbf[:], -1.0, None, op0=ALU.mult), r=[nbf.name], w=[nbf.name])
        with contextlib.ExitStack() as t:
            hT = self.sb(t, "hT", [128, KC, 1024], BF16)
            wsl = [self.sb(t, "wm", [128, KC, 256], BF16) for _ in range(2)]
            xpad = [self.sb(t, "xpad", [128, 3 + 1024], F32) for _ in range(2)]
            acc = [self.sb(t, "acc", [128, 1024], F32) for _ in range(2)]
            carry = self.sb(t, "carry", [128, 8, 3], F32)
            ogt = [self.sb(t, "ogt", [128, 512], BF16) for _ in range(2)]
            pp = [self.ps(t, "pp", [128, 512], F32) for _ in range(3)]
            pv = self.ps(t, "pvl", [128, 512], F32)
            pg = self.ps(t, "pg", [128, 512], F32)
            xc, sqb, rstd, pss = self.norm_tmp(t)
            sc.pool(lambda: nc.gpsimd.memset(carry[:], 0.0), w=[carry.name])
            wi = 0
            pi = 0
            oi = 0
            for blk in range(2):
                tok0 = blk * 1024
                self.norm_block(t, (hT, 0), lambda c: [("hT", c)], tok0, "mix", l, xc, sqb, rstd, pss)
                hk = lambda kc: ("hT", kc)
                for j in range(4):
                    ws = wsl[wi % 2]
                    wi += 1
                    c0 = 1088 + j * 256
                    sc.dma("pool", lambda ws=ws, c0=c0: nc.gpsimd.dma_start(out=ws[:], in_=w_in[:, :, c0:c0 + 256]), [], [ws.name], None)
                    for cc in range(2):
                        ch = j * 2 + cc
                        xp = xpad[ch % 2]
                        ac = acc[ch % 2]
                        sc.dve(lambda xp=xp, ch=ch: nc.vector.tensor_copy(xp[:, 0:3], carry[:, ch, :]), r=[carry.name], w=[xp.name])
                        for th in range(2):
                            p = pp[pi % 3]
                            pi += 1
                            for kc in range(KC):
                                sc.pe(lambda p=p, ws=ws, cc=cc, kc=kc, th=th: nc.tensor.matmul(
                                    p[:], ws[:, kc, cc * 128:(cc + 1) * 128], hT[:, kc, th * 512:(th + 1) * 512],
                                    start=(kc == 0), stop=(kc == KC - 1)), r=[ws.name, hk(kc)], w=[p.name])
                            sc.act(lambda p=p, xp=xp, th=th: nc.scalar.copy(xp[:, 3 + th * 512:3 + (th + 1) * 512], p[:]), r=[p.name], w=[xp.name])
                        sc.dve(lambda xp=xp, ch=ch: nc.vector.tensor_copy(carry[:, ch, :], xp[:, 1024:1027]), r=[xp.name], w=[carry.name])
                        eng = sc.dve
                        e_ = nc.vector
                        for jj in range(4):
                            wcol = self.gcol("convw", i, jj * 8 + ch)
                            if jj == 0:
                                eng(lambda e_=e_, ac=ac, xp=xp, wcol=wcol: e_.tensor_scalar(ac[:], xp[:, 0:1024], wcol, None, op0=ALU.mult),
                                    r=[xp.name, "gains"], w=[ac.name])
                            else:
                                eng(lambda e_=e_, ac=ac, xp=xp, wcol=wcol, jj=jj: e_.scalar_tensor_tensor(
                                    out=ac[:], in0=xp[:, jj:jj + 1024], scalar=wcol, in1=ac[:], op0=ALU.mult, op1=ALU.add),
                                    r=[xp.name, ac.name, "gains"], w=[ac.name])
                        bcol = self.gcol("convb", i, ch)
                        sc.act(lambda ac=ac, bcol=bcol: nc.scalar.activation(out=ac[:], in_=ac[:], func=AF.Silu, bias=bcol, scale=1.0),
                               r=[ac.name, "gains"], w=[ac.name])
                        if ch < 4:
                            eng(lambda e_=e_, ac=ac, ch=ch, tok0=tok0: e_.tensor_copy(MQ[:, ch, tok0:tok0 + 1024], ac[:]),
                                r=[ac.name], w=[("MQ", ch, blk)])
                        else:
                            eng(lambda e_=e_, ac=ac, ch=ch, tok0=tok0: e_.tensor_scalar(MK[:, ch - 4, tok0:tok0 + 1024], ac[:], 128.0 ** -0.5, None, op0=ALU.mult),
                                r=[ac.name], w=[("MK", ch - 4, blk)])
                for h in range(4):
                    ws = wsl[wi % 2]
                    wi += 1
                    c0 = 2112 + h * 256
                    sc.dma("pool", lambda ws=ws, c0=c0: nc.gpsimd.dma_start(out=ws[:], in_=w_in[:, :, c0:c0 + 256]), [], [ws.name], None)
                    for tt in range(8):
                        for kc in range(KC):
                            sc.pe(lambda ws=ws, kc=kc, tt=tt: nc.tensor.matmul(
                                pv[:, 0:256], hT[:, kc, tt * 128:(tt + 1) * 128], ws[:, kc, :], start=(kc == 0), stop=(kc == KC - 1)),
                                r=[ws.name, hk(kc)], w=[pv.name])
                        sc.act(lambda tt=tt, h=h, blk=blk: nc.scalar.copy(VL[:, blk * 8 + tt, h * 256:(h + 1) * 256], pv[:, 0:256]),
                               r=[pv.name], w=[("VL", blk * 8 + tt, h)])
                ws = wsl[wi % 2]
                wi += 1
                sc.dma("pool", lambda ws=ws: nc.gpsimd.dma_start(out=ws[:, :, 0:8], in_=w_in[:, :, 3136:3144]), [], [ws.name], None)
                for gi, G in ((0, G1), (1, G2)):
                    for th in range(2):
                        for kc in range(KC):
                            sc.pe(lambda ws=ws, kc=kc, th=th, gi=gi: nc.tensor.matmul(
                                pg[0:4, :], ws[:, kc, gi * 4:gi * 4 + 4], hT[:, kc, th * 512:(th + 1) * 512],
                                start=(kc == 0), stop=(kc == KC - 1)), r=[ws.name, hk(kc)], w=[pg.name])
                        sc.act(lambda G=G, th=th, tok0=tok0: nc.scalar.copy(G[:, tok0 + th * 512:tok0 + (th + 1) * 512], pg[0:4, :]),
                               r=[pg.name], w=[G.name])
                for j in range(4):
                    ws = wsl[wi % 2]
                    wi += 1
                    c0 = 3144 + j * 256
                    sc.dma("pool", lambda ws=ws, c0=c0: nc.gpsimd.dma_start(out=ws[:], in_=w_in[:, :, c0:c0 + 256]), [], [ws.name], None)
                    for cc in range(2):
                        ch = j * 2 + cc
                        for th in range(2):
                            p = pp[pi % 3]
                            pi += 1
                            og_ = ogt[oi % 2]
                            oi += 1
                            for kc in range(KC):
                                sc.pe(lambda p=p, ws=ws, cc=cc, kc=kc, th=th: nc.tensor.matmul(
                                    p[:], ws[:, kc, cc * 128:(cc + 1) * 128], hT[:, kc, th * 512:(th + 1) * 512],
                                    start=(kc == 0), stop=(kc == KC - 1)), r=[ws.name, hk(kc)], w=[p.name])
                            sc.act(lambda p=p, og_=og_: nc.scalar.activation(out=og_[:], in_=p[:], func=AF.Sigmoid), r=[p.name], w=[og_.name])
                            t0 = tok0 + th * 512
                            sc.dma("sp", lambda og_=og_, ch=ch, t0=t0: nc.sync.dma_start(out=ogv[:, ch, t0:t0 + 512], in_=og_[:]),
                                   [og_.name], [("og_d", t0 // 512)], None)
            sc.barrier()
        zer = self.sb(st, "zer", [4, 128], F32)
        amax = self.sb(st, "amax", [4, NCH], F32)
        bL = self.sb(st, "bL", [4, NCH], F32)
        mseq = self.sb(st, "mseq", [4, NCH], F32)
        mprev = self.sb(st, "mprev", [4, NCH], F32)
        nMref = self.sb(st, "nMref", [4, NCH], F32)
        dec = self.sb(st, "dec", [4, NCH], F32)
        X4 = self.sb(st, "X4", [4, 4, NCH], F32)
        SEL = self.sb(st, "SEL", [4, 4, 128], F32)
        dec_rep = self.sb(st, "dec_rep", [128, 64], F32)
        u_tm = self.sb(st, "u_tm", [128, 64], F32)
        tri = self.sb(st, "tri", [128, 128], F32)
        Cst = self.sb(st, "Cst", [128, 4, 257], F32)
        Cd = self.sb(st, "Cd", [128, 4, 256], BF16)
        ndf = self.sb(st, "ndf", [128, 4], F32)
        ndr = self.sb(st, "ndr", [128, 4, 128], BF16)
        WmT = [self.sb(st, "WmT", [128, 128], BF16) for _ in range(3)]
        ku = [self.sb(st, "ku", [128, 128], BF16) for _ in range(3)]
        flo = [self.sb(st, "flo", [128, 128], F32) for _ in range(2)]
        dm = [self.sb(st, "dm", [128, 128], F32) for _ in range(2)]
        tmpn = [self.sb(st, "tmpn", [128, 128], F32) for _ in range(3)]
        ogc = [self.sb(st, "ogc", [128, 8, 128], BF16) for _ in range(2)]
        hout = [self.sb(st, "hout", [128, 8, 128], BF16) for _ in range(2)]
        pA = [self.ps(st, "pA", [128, 512], F32) for _ in range(2)]
        pB = [self.ps(st, "pB", [128, 512], F32) for _ in range(2)]
        pC = [self.ps(st, "pC", [128, 512], F32) for _ in range(2)]
        pT = self.ps(st, "pT", [128, 1024], BF16)
        pR = self.ps(st, "pR", [128, 512], F32)
        g1, g2, g3 = G1.name, G2.name, G3.name
        sc.pool(lambda: nc.gpsimd.memset(zer[:], 0.0), w=[zer.name])
        sc.pool(lambda: nc.gpsimd.memset(Cst[:], 0.0), w=["Cst"])
        sc.pool(lambda: nc.gpsimd.memset(Cd[:], 0.0), w=["Cd"])
        sc.pool(lambda: nc.gpsimd.memset(ndr[:], 0.0), w=["ndr"])
        sc.pool(lambda: nc.gpsimd.affine_select(tri[:], ones_f[:], [[1, 128]], ALU.is_ge, 0.0, base=0, channel_multiplier=-1),
                r=["ones_f"], w=[tri.name])
        sc.dve(lambda: nc.vector.tensor_scalar(G1[:], G1[:], bi[:, 0:1], None, op0=ALU.add), r=[g1, bi.name], w=[g1])
        sc.act(lambda: nc.scalar.activation(out=G2[:], in_=G2[:], func=AF.Exp, bias=nbf[:, 0:1], scale=-1.0), r=[g2, nbf.name], w=[g2])
        sc.dve(lambda: nc.vector.tensor_scalar(G2[:], G2[:], 1.0, None, op0=ALU.add), r=[g2], w=[g2])
        sc.act(lambda: nc.scalar.activation(out=G2[:], in_=G2[:], func=AF.Ln), r=[g2], w=[g2])
        sc.dve(lambda: nc.vector.tensor_scalar(G2[:], G2[:], -1.0, None, op0=ALU.mult), r=[g2], w=[g2])
        for c in range(NCH):
            sc.dve(lambda c=c: nc.vector.tensor_tensor_scan(G3[:, c * 128:(c + 1) * 128], zer[:], G2[:, c * 128:(c + 1) * 128], 0.0, ALU.add, ALU.add),
                   r=[g2, zer.name], w=[g3])
        sc.dve(lambda: nc.vector.tensor_sub(G1[:], G1[:], G3[:]), r=[g1, g3], w=[g1])
        sc.dve(lambda: nc.vector.tensor_reduce(out=amax[:], in_=G1[:].rearrange("p (c t) -> p c t", t=128), axis=AX.X, op=ALU.max),
               r=[g1], w=[amax.name])
        sc.dve(lambda: nc.vector.tensor_copy(bL[:], G3[:].rearrange("p (c t) -> p c t", t=128)[:, :, 127]), r=[g3], w=[bL.name])
        sc.dve(lambda: nc.vector.tensor_tensor_scan(mseq[:], amax[:], bL[:], 0.0, ALU.max, ALU.add), r=[amax.name, bL.name], w=[mseq.name])
        sc.dve(lambda: nc.vector.memset(mprev[:], 0.0), w=[mprev.name])
        sc.dve(lambda: nc.vector.tensor_copy(mprev[:, 1:NCH], mseq[:, 0:NCH - 1]), r=[mseq.name, mprev.name], w=[mprev.name])
        sc.dve(lambda: nc.vector.tensor_tensor(nMref[:], mprev[:], amax[:], ALU.max), r=[mprev.name, amax.name], w=[nMref.name])
        sc.dve(lambda: nc.vector.tensor_sub(dec[:], mprev[:], nMref[:]), r=[mprev.name, nMref.name], w=[dec.name])
        sc.act(lambda: nc.scalar.activation(out=dec[:], in_=dec[:], func=AF.Exp), r=[dec.name], w=[dec.name])
        sc.dve(lambda: nc.vector.tensor_scalar(nMref[:], nMref[:], -1.0, None, op0=ALU.mult), r=[nMref.name, dec.name], w=[nMref.name])
        for c in range(NCH):
            sc.act(lambda c=c: nc.scalar.activation(out=G1[:, c * 128:(c + 1) * 128], in_=G1[:, c * 128:(c + 1) * 128], func=AF.Exp,
                                                    bias=nMref[:, c:c + 1], scale=1.0), r=[g1, nMref.name, amax.name], w=[g1])
            sc.act(lambda c=c: nc.scalar.activation(out=G3[:, c * 128:(c + 1) * 128], in_=G3[:, c * 128:(c + 1) * 128], func=AF.Exp,
                                                    bias=nMref[:, c:c + 1], scale=-1.0), r=[g3, nMref.name, bL.name], w=[g3])
        for h in range(4):
            sc.dve(lambda h=h: nc.vector.tensor_scalar(X4[:, h, :], dec[:], ident_f[0:4, h:h + 1], None, op0=ALU.mult),
                   r=[dec.name, "ident_f"], w=[X4.name])
            sc.dve(lambda h=h: nc.vector.tensor_scalar(SEL[:, h, :], ones_f[0:4, :], ident_f[0:4, h:h + 1], None, op0=ALU.mult),
                   r=["ones_f", "ident_f"], w=[SEL.name])
        sc.pe(lambda: nc.tensor.matmul(pR[:, 0:64], ones_f[0:4, :], X4[:].rearrange("p a b -> p (a b)"), start=True, stop=True),
              r=[X4.name, "ones_f"], w=[pR.name])
        sc.dve(lambda: nc.vector.tensor_copy(dec_rep[:], pR[:, 0:64]), r=[pR.name], w=[dec_rep.name])
        for c in range(NCH):
            sc.pe(lambda c=c: nc.tensor.matmul(pR[:, 64 + c * 4:64 + c * 4 + 4], G1[:, c * 128:(c + 1) * 128], ident_f[0:4, 0:4], start=True, stop=True),
                  r=[g1, "ident_f"], w=[pR.name])
        sc.dve(lambda: nc.vector.tensor_copy(u_tm[:], pR[:, 64:128]), r=[pR.name], w=[u_tm.name])
        k = 0
        for c in range(NCH):
            cs = slice(c * 128, (c + 1) * 128)
            og_ = ogc[c % 2]
            ho = hout[c % 2]
            sc.dma("sp", lambda og_=og_, cs=cs: nc.sync.dma_start(out=og_[:], in_=ogv[:, :, cs]), [("og_d", c // 4)], [og_.name], None)
            for h in range(4):
                a_ = pA[k % 2]
                b_ = pB[k % 2]
                c_ = pC[k % 2]
                wm = WmT[k % 3]
                ku_ = ku[k % 3]
                fl = flo[k % 2]
                dm_ = dm[k % 2]
                ucol = u_tm[:, c * 4 + h:c * 4 + h + 1]
                blk = c // 8
                sc.pe(lambda a_=a_, h=h, cs=cs: nc.tensor.matmul(a_[:, 0:128], MK[:, h, cs], MQ[:, h, cs], start=True, stop=True),
                      r=[("MK", h, blk), ("MQ", h, blk)], w=[a_.name])
                sc.dve(lambda a_=a_, wm=wm, ucol=ucol: nc.vector.scalar_tensor_tensor(
                    out=wm[:], in0=a_[:, 0:128], scalar=ucol, in1=tri[:], op0=ALU.mult, op1=ALU.mult),
                    r=[a_.name, u_tm.name, tri.name], w=[wm.name])
                for j in range(2):
                    sc.pe(lambda b_=b_, wm=wm, h=h, j=j, c=c: nc.tensor.matmul(
                        b_[:, j * 128:(j + 1) * 128], VL[:, c, h * 256 + j * 128:h * 256 + (j + 1) * 128], wm[:], start=True, stop=False),
                        r=[wm.name, ("VL", c, h)], w=[b_.name])
                    sc.pe(lambda b_=b_, h=h, j=j, cs=cs: nc.tensor.matmul(
                        b_[:, j * 128:(j + 1) * 128], Cd[:, h, j * 128:(j + 1) * 128], MQ[:, h, cs], start=False, stop=True),
                        r=[("Cd", h), ("MQ", h, blk)], w=[b_.name])
                sc.pe(lambda b_=b_, wm=wm: nc.tensor.matmul(b_[:, 256:384], ones_b[:], wm[:], start=True, stop=False),
                      r=[wm.name, "ones_b"], w=[b_.name])
                sc.pe(lambda b_=b_, h=h, cs=cs: nc.tensor.matmul(b_[:, 256:384], ndr[:, h, :], MQ[:, h, cs], start=False, stop=True),
                      r=[("ndr", h), ("MQ", h, blk)], w=[b_.name])
                sc.pe(lambda b_=b_, h=h, cs=cs: nc.tensor.matmul(b_[:, 384:512], SEL[:, h, :], G3[:, cs], start=True, stop=True),
                      r=[SEL.name, g3], w=[b_.name])
                sc.act(lambda b_=b_, fl=fl: nc.scalar.copy(fl[:], b_[:, 384:512]), r=[b_.name], w=[fl.name])
                sc.act(lambda b_=b_, dm_=dm_: nc.scalar.activation(out=dm_[:], in_=b_[:, 256:384], func=AF.Abs), r=[b_.name], w=[dm_.name])
                sc.dve(lambda fl=fl, dm_=dm_: nc.vector.tensor_tensor(dm_[:], dm_[:], fl[:], ALU.max),
                       r=[dm_.name, fl.name], w=[dm_.name])
                sc.dve(lambda dm_=dm_: nc.vector.reciprocal(dm_[:], dm_[:]), r=[dm_.name], w=[dm_.name])
                for j in range(2):
                    tn = tmpn[(2 * k + j) % 3]
                    sc.dve(lambda b_=b_, tn=tn, dm_=dm_, j=j: nc.vector.tensor_tensor(tn[:], b_[:, j * 128:(j + 1) * 128], dm_[:], ALU.mult),
                           r=[b_.name, dm_.name], w=[tn.name])
                    sc.pool(lambda tn=tn, ho=ho, og_=og_, h=h, j=j: nc.gpsimd.tensor_tensor(ho[:, 2 * h + j, :], tn[:], og_[:, 2 * h + j, :], ALU.mult),
                            r=[tn.name, og_.name], w=[(ho.name, h)])
                sc.pe(lambda h=h, cs=cs: nc.tensor.transpose(pT[:, 0:128], MK[:, h, cs], ident_b[:]), r=[("MK", h, blk), "ident_b"], w=[pT.name])
                sc.dve(lambda ku_=ku_, ucol=ucol: nc.vector.tensor_scalar(ku_[:], pT[:, 0:128], ucol, None, op0=ALU.mult),
                       r=[pT.name, u_tm.name], w=[ku_.name])
                sc.pe(lambda c_=c_, ku_=ku_, h=h, c=c: nc.tensor.matmul(c_[:, 0:256], ku_[:], VL[:, c, h * 256:(h + 1) * 256], start=True, stop=True),
                      r=[ku_.name, ("VL", c, h)], w=[c_.name])
                sc.pe(lambda c_=c_, ku_=ku_: nc.tensor.matmul(c_[:, 256:257], ku_[:], ones_b[:, 0:1], start=True, stop=True),
                      r=[ku_.name, "ones_b"], w=[c_.name])
                dcol = dec_rep[:, h * 16 + c:h * 16 + c + 1]
                sc.dve(lambda c_=c_, h=h, dcol=dcol: nc.vector.scalar_tensor_tensor(
                    out=Cst[:, h, :], in0=Cst[:, h, :], scalar=dcol, in1=c_[:, 0:257], op0=ALU.mult, op1=ALU.add),
                    r=[c_.name, ("Cst", h), dec_rep.name], w=[("Cst", h)])
                if c + 1 < NCH:
                    dnext = dec_rep[:, h * 16 + c + 1:h * 16 + c + 2]
                    sc.dve(lambda h=h, dnext=dnext: nc.vector.tensor_scalar(Cd[:, h, :], Cst[:, h, 0:256], dnext, None, op0=ALU.mult),
                           r=[("Cst", h), dec_rep.name], w=[("Cd", h)])
                    sc.dve(lambda h=h, dnext=dnext: nc.vector.tensor_scalar(ndf[:, h:h + 1], Cst[:, h, 256:257], dnext, None, op0=ALU.mult),
                           r=[("Cst", h), dec_rep.name], w=[("ndf", h)])
                    sc.dve(lambda h=h: nc.vector.tensor_scalar(ndr[:, h, :], ones_f[:], ndf[:, h:h + 1], None, op0=ALU.mult),
                           r=[("ndf", h), "ones_f"], w=[("ndr", h)])
                k += 1
            sc.dma("sp", lambda ho=ho, cs=cs: nc.sync.dma_start(out=oTv[:, 8:16, cs], in_=ho[:]),
                   [(ho.name, h) for h in range(4)], [("oT_d", 8 + c2) for c2 in range(8)], None)
        sc.barrier()


def _stage_even(self, l):
    self.mla(l)
    self.mlstm(l)
    self.out_proj(self.inp("ev_w_out", l // 2), l)


Builder.mlstm = _mlstm
Builder.stage_even = _stage_even
```

```python
import contextlib
import math
import numpy as np
import concourse.bass as bass
import concourse.mybir as mybir
from concourse.bass_utils import run_bass_kernel_spmd

F32 = mybir.dt.float32
BF16 = mybir.dt.bfloat16
I32 = mybir.dt.int32
AF = mybir.ActivationFunctionType
ALU = mybir.AluOpType
AX = mybir.AxisListType

S = 2048
D = 2048
KC = D // 128
DFF = 8192
DEPTH = 4
EPS = 1e-6
EVEN_IN = 4168
NCORES = 8


class _Op:
    __slots__ = ("eng", "fn", "reads", "writes", "dma", "grp", "deps", "signal", "sem", "val", "bar")


class Sched:
    NPOOL = {"sp": 24, "pool": 6, "act": 8}

    def __init__(self, nc, es):
        self.nc = nc
        self.es = es
        self.ops = []
        self.engs = {"pe": nc.tensor, "act": nc.scalar, "dve": nc.vector, "pool": nc.gpsimd, "sp": nc.sync}

    def add(self, eng, fn, reads=(), writes=(), dma=False, grp=None, bar=False):
        o = _Op()
        o.eng = eng
        o.fn = fn
        reads = list(reads)
        writes = list(writes)
        for k in list(reads):
            if isinstance(k, str) and k.startswith("PS_"):
                reads.remove(k)
                if k not in writes:
                    writes.append(k)
        o.reads = tuple(reads)
        o.writes = tuple(writes)
        o.dma = dma
        o.grp = grp
        o.bar = bar
        o.deps = None
        o.signal = dma
        o.sem = None
        o.val = None
        self.ops.append(o)

    def pe(self, fn, r=(), w=()):
        self.add("pe", fn, r, w)

    def act(self, fn, r=(), w=()):
        self.add("act", fn, r, w)

    def dve(self, fn, r=(), w=()):
        self.add("dve", fn, r, w)

    def pool(self, fn, r=(), w=()):
        self.add("pool", fn, r, w)

    def dma(self, eng, fn, r, w, grp):
        self.add(eng, fn, r, w, dma=True, grp=grp)

    def barrier(self):
        self.add("sp", lambda: self.nc.sync.nop(), (), (), bar=True)

    def finalize(self):
        ops = self.ops
        nc = self.nc
        last_w = {}
        rd_eng = {}
        rd_dma = {}
        last_on_eng = {}
        last_dma_grp = {}
        dma_rr = {}
        cur_bar = None
        for i, o in enumerate(ops):
            deps = set()
            if o.bar:
                deps.update(last_on_eng.values())
                deps.update(last_dma_grp.values())
                if cur_bar is not None:
                    deps.add(cur_bar)
            else:
                if cur_bar is not None:
                    deps.add(cur_bar)
                for k in o.reads:
                    if k in last_w:
                        deps.add(last_w[k])
                for k in o.writes:
                    if k in last_w:
                        deps.add(last_w[k])
                    if k in rd_eng:
                        deps.update(rd_eng[k].values())
                    if k in rd_dma:
                        deps.update(rd_dma[k])
                for k in o.reads:
                    if o.dma:
                        rd_dma.setdefault(k, []).append(i)
                    else:
                        rd_eng.setdefault(k, {})[o.eng] = i
                for k in o.writes:
                    last_w[k] = i
                    rd_eng.pop(k, None)
                    rd_dma.pop(k, None)
            if o.dma:
                n = dma_rr.get(o.eng, 0)
                dma_rr[o.eng] = n + 1
                o.grp = (o.eng, n % self.NPOOL[o.eng])
                if o.grp in last_dma_grp:
                    deps.add(last_dma_grp[o.grp])
            deps.discard(i)
            if o.eng == "pe" and not o.dma:
                deps = {d for d in deps if not (ops[d].eng == "pe" and not ops[d].dma)}
            o.deps = deps
            if o.bar:
                cur_bar = i
                last_on_eng = {}
            if o.dma:
                last_dma_grp[o.grp] = i
            else:
                last_on_eng[o.eng] = i
        for o in ops:
            for d in o.deps:
                ops[d].signal = True
        eng_sem = {}
        eng_cnt = {}
        grp_sem = {}
        grp_cnt = {}
        for e in self.engs:
            eng_sem[e] = self.es.enter_context(nc.semaphore("sem_" + e))
            eng_cnt[e] = 0
        for o in ops:
            if o.dma:
                if o.grp not in grp_sem:
                    grp_sem[o.grp] = self.es.enter_context(nc.semaphore("semd_%d" % len(grp_sem)))
                    grp_cnt[o.grp] = 0
                grp_cnt[o.grp] += 16
                o.sem = grp_sem[o.grp]
                o.val = grp_cnt[o.grp]
            elif o.signal:
                eng_cnt[o.eng] += 1
                o.sem = eng_sem[o.eng]
                o.val = eng_cnt[o.eng]
        waited = {e: {} for e in self.engs}
        nwait = 0
        for o in ops:
            need = {}
            for d in o.deps:
                od = ops[d]
                key = od.sem.name
                if key not in need or need[key][1] < od.val:
                    need[key] = (od.sem, od.val)
            w = waited[o.eng]
            eng = self.engs[o.eng]
            for key, (sem, val) in need.items():
                if w.get(key, 0) >= val:
                    continue
                eng.wait_ge(sem, val)
                w[key] = val
                nwait += 1
            inst = o.fn()
            if o.signal:
                inst.then_inc(o.sem, 16 if o.dma else 1)
        sp = self.engs["sp"]
        for g, sem in grp_sem.items():
            if waited["sp"].get(sem.name, 0) < grp_cnt[g]:
                sp.wait_ge(sem, grp_cnt[g])
        self.stats = (len(ops), nwait, len(grp_sem))


class Builder:
    def __init__(self, cfg):
        self.cfg = cfg
        self.nc = bass.Bass("TRN2", target_bir_lowering=False)
        self.es = contextlib.ExitStack()
        self.sc = Sched(self.nc, self.es)
        self.uid = 0
        self.inputs = {}

    def sb(self, st, name, shape, dt):
        self.uid += 1
        return st.enter_context(self.nc.sbuf_tensor("%s_%d" % (name, self.uid), list(shape), dt))

    def ps(self, st, name, shape, dt):
        self.uid += 1
        return st.enter_context(self.nc.psum_tensor("PS_%s_%d" % (name, self.uid), list(shape), dt))

    SHAPES = {"x": [S, D], "norm_mix": [DEPTH, D], "norm_mlp": [DEPTH, D], "ev_w_in": [D, EVEN_IN],
              "mla_q_norm": [2, 512], "mla_w_uq": [512, 1536], "mla_kv_norm": [2, 512], "mla_w_ukv": [512, 2048],
              "mlstm_conv_w": [2, 4, 1024], "mlstm_conv_b": [2, 1024], "mlstm_b_i": [2, 4], "mlstm_b_f": [2, 4],
              "ev_w_out": [D, D], "od_w_qkv": [D, 3 * D], "od_w_out": [D, D], "mlp_w1": [D, DFF],
              "mlp_w2": [DFF, D], "norm_final": [D]}
    PER_LAYER = ("ev_w_in", "mla_w_uq", "mla_w_ukv", "ev_w_out", "od_w_qkv", "od_w_out", "mlp_w1", "mlp_w2")

    def inp(self, name, idx=None):
        key = name if idx is None else "%s_%d" % (name, idx)
        if key not in self.inputs:
            self.inputs[key] = self.nc.dram_tensor(key, list(self.SHAPES[name]), F32, kind="ExternalInput").ap()
        return self.inputs[key]

    def build(self):
        nc = self.nc
        sc = self.sc
        cfg = self.cfg
        self.x_d = self.inp("x")
        self.y_d = nc.dram_tensor("y", [S, D], F32, kind="ExternalOutput").ap()
        self.xT_d = nc.dram_tensor("xT_scr", [D, S], F32).ap()
        self.oT_d = nc.dram_tensor("oT_scr", [D, S], BF16).ap()
        self.og_d = nc.dram_tensor("og_scr", [1024, S], BF16).ap()
        self.dbg_d = None
        if cfg.get("dbg"):
            self.dbg_d = nc.dram_tensor("dbg", [D, S], F32, kind="ExternalOutput").ap()

        self.gst = self.es
        self.setup_consts()
        if not cfg.get("skip_load"):
            self.stage_load_x()
        for st in cfg["stages"]:
            kind, l = st
            if kind == "mlp":
                self.stage_mlp(l)
            elif kind == "odd":
                self.stage_odd(l)
            elif kind == "even":
                self.stage_even(l)
        if self.dbg_d is not None:
            self.stage_dbg()
        if not cfg.get("skip_final"):
            self.stage_final()
        sc.finalize()
        return nc

    def setup_consts(self):
        nc, sc = self.nc, self.sc
        g = self.gst
        self.ones_f = self.sb(g, "ones_f", [128, 128], F32)
        self.ones_b = self.sb(g, "ones_b", [128, 128], BF16)
        self.ident_f = self.sb(g, "ident_f", [128, 128], F32)
        self.ident_b = self.sb(g, "ident_b", [128, 128], BF16)
        self.gains = self.sb(g, "gains", [128, 256], F32)
        ones_f, ones_b, ident_f, ident_b = self.ones_f, self.ones_b, self.ident_f, self.ident_b
        sc.pool(lambda: nc.gpsimd.memset(ones_f[:], 1.0), w=["ones_f"])
        sc.pool(lambda: nc.gpsimd.memset(ones_b[:], 1.0), w=["ones_b"])
        sc.pool(lambda: nc.gpsimd.affine_select(ident_f[:], ones_f[:], [[-1, 128]], ALU.is_equal, 0.0,
                                                base=0, channel_multiplier=1), r=["ones_f"], w=["ident_f"])
        sc.pool(lambda: nc.gpsimd.tensor_copy(ident_b[:], ident_f[:]), r=["ident_f"], w=["ident_b"])
        with contextlib.ExitStack() as st:
            p1 = self.sb(st, "p1", [128, 128], F32)
            p2 = self.sb(st, "p2", [128, 128], F32)
            pt = self.ps(st, "pt", [128, 512], F32)
            sc.pool(lambda: nc.gpsimd.memset(p2[:], 0.0), w=["p2"])
            sc.dma("sp", lambda: nc.sync.dma_start(out=p1[0:64, :], in_=self.inp("norm_mix").rearrange("l (c p) -> (l c) p", p=128)),
                   [], ["p1"], "p1")
            sc.dma("sp", lambda: nc.sync.dma_start(out=p1[64:128, :], in_=self.inp("norm_mlp").rearrange("l (c p) -> (l c) p", p=128)),
                   [], ["p1"], "p1")
            sc.dma("sp", lambda: nc.sync.dma_start(out=p2[0:16, :], in_=self.inp("norm_final").rearrange("(c p) -> c p", p=128)),
                   ["p2"], ["p2"], "p2")
            sc.dma("sp", lambda: nc.sync.dma_start(out=p2[16:24, :], in_=self.inp("mla_q_norm").rearrange("l (c p) -> (l c) p", p=128)),
                   ["p2"], ["p2"], "p2")
            sc.dma("sp", lambda: nc.sync.dma_start(out=p2[24:32, :], in_=self.inp("mla_kv_norm").rearrange("l (c p) -> (l c) p", p=128)),
                   ["p2"], ["p2"], "p2")
            sc.dma("sp", lambda: nc.sync.dma_start(out=p2[32:48, :], in_=self.inp("mlstm_conv_b").rearrange("l (c p) -> (l c) p", p=128)),
                   ["p2"], ["p2"], "p2")
            sc.dma("sp", lambda: nc.sync.dma_start(out=p2[48:112, :], in_=self.inp("mlstm_conv_w").rearrange("l j (c p) -> (l j c) p", p=128)),
                   ["p2"], ["p2"], "p2")
            sc.pe(lambda: nc.tensor.transpose(pt[:, 0:128], p1[:], ident_f[:]), r=["p1", "ident_f"], w=["pt"])
            sc.pe(lambda: nc.tensor.transpose(pt[:, 128:256], p2[:], ident_f[:]), r=["p2", "ident_f"], w=["pt"])
            gains = self.gains
            sq = math.sqrt(D)
            sc.dve(lambda: nc.vector.tensor_scalar(gains[:, 0:144], pt[:, 0:144], sq, None, op0=ALU.mult),
                   r=["pt"], w=["gains"])
            sc.dve(lambda: nc.vector.tensor_scalar(gains[:, 144:160], pt[:, 144:160], math.sqrt(512.0), None, op0=ALU.mult),
                   r=["pt"], w=["gains"])
            sc.dve(lambda: nc.vector.tensor_copy(gains[:, 160:240], pt[:, 160:240]), r=["pt"], w=["gains"])
            sc.barrier()

    def gcol(self, kind, l, c):
        if kind == "mix":
            j = l * 16 + c
        elif kind == "mlp":
            j = 64 + l * 16 + c
        elif kind == "final":
            j = 128 + c
        elif kind == "qn":
            j = 144 + l * 4 + c
        elif kind == "kvn":
            j = 152 + l * 4 + c
        elif kind == "convb":
            j = 160 + l * 8 + c
        elif kind == "convw":
            j = 176 + l * 32 + c
        return self.gains[:, j:j + 1]

    def rstd_from(self, out_ap, ps_ap, rkeys, wkeys, eps_total):
        nc, sc = self.nc, self.sc
        sc.dve(lambda: nc.vector.tensor_scalar(out_ap, ps_ap, eps_total, None, op0=ALU.add), r=rkeys, w=wkeys)
        sc.act(lambda: nc.scalar.activation(out=out_ap, in_=out_ap, func=AF.Sqrt), r=wkeys, w=wkeys)
        sc.dve(lambda: nc.vector.reciprocal(out_ap, out_ap), r=wkeys, w=wkeys)

    def stage_load_x(self):
        nc, sc = self.nc, self.sc
        xTv = self.xT_d.rearrange("(c p) t -> p c t", p=128)
        with contextlib.ExitStack() as st:
            xin = [self.sb(st, "xin", [128, D], F32) for _ in range(2)]
            xo = [self.sb(st, "xo", [128, KC, 128], F32) for _ in range(2)]
            pb = [self.ps(st, "pb", [128, 512], F32) for _ in range(4)]
            ident_f = self.ident_f
            for tt in range(16):
                xi = xin[tt % 2]
                xx = xo[tt % 2]
                sc.dma("sp", lambda xi=xi, tt=tt: nc.sync.dma_start(out=xi[:], in_=self.x_d[tt * 128:(tt + 1) * 128, :]),
                       [], [xi.name], xi.name)
                for q in range(4):
                    p = pb[q]
                    for j in range(4):
                        c = q * 4 + j
                        sc.pe(lambda p=p, j=j, c=c, xi=xi: nc.tensor.transpose(p[:, j * 128:(j + 1) * 128], xi[:, c * 128:(c + 1) * 128], ident_f[:]),
                              r=[xi.name, "ident_f"], w=[p.name])
                    if q % 2 == 0:
                        sc.act(lambda p=p, q=q, xx=xx: nc.scalar.copy(xx[:, q * 4:(q + 1) * 4, :], p[:].rearrange("p (a b) -> p a b", b=128)),
                               r=[p.name], w=[xx.name])
                    else:
                        sc.dve(lambda p=p, q=q, xx=xx: nc.vector.tensor_copy(xx[:, q * 4:(q + 1) * 4, :], p[:].rearrange("p (a b) -> p a b", b=128)),
                               r=[p.name], w=[xx.name])
                sc.dma("sp", lambda xx=xx, tt=tt: nc.sync.dma_start(out=xTv[:, :, tt * 128:(tt + 1) * 128], in_=xx[:]),
                       [xx.name], [("xT", c, tt // 4) for c in range(KC)], "xT")
            sc.barrier()

    def norm_block(self, st_tmp, hT, hkeys, tok0, gkind, l, xc, sqb, rstd, pss):
        nc, sc = self.nc, self.sc
        xTv = self.xT_d.rearrange("(c p) t -> p c t", p=128)
        ones_f = self.ones_f
        NT = 1024
        tb0 = tok0 // 512
        hT_t, col0 = hT
        for c in range(KC):
            x_ = xc[c % len(xc)]
            s_ = sqb[c % len(sqb)]
            sc.dma("sp", lambda x_=x_, c=c: nc.sync.dma_start(out=x_[:], in_=xTv[:, c, tok0:tok0 + NT]),
                   [("xT", c, tb0), ("xT", c, tb0 + 1)], [x_.name], x_.name)
            sc.act(lambda x_=x_, s_=s_: nc.scalar.activation(out=s_[:], in_=x_[:], func=AF.Square), r=[x_.name], w=[s_.name])
            for h in range(2):
                sc.pe(lambda h=h, s_=s_, c=c: nc.tensor.matmul(pss[h][:], ones_f[:], s_[:, h * 512:(h + 1) * 512],
                                                              start=(c == 0), stop=(c == KC - 1)),
                      r=[s_.name, "ones_f"], w=[pss[h].name])
        for h in range(2):
            self.rstd_from(rstd[:, h * 512:(h + 1) * 512], pss[h][:], [pss[h].name], [rstd.name], float(D * EPS))
        for c in range(KC):
            x_ = xc[c % len(xc)]
            sc.dma("sp", lambda x_=x_, c=c: nc.sync.dma_start(out=x_[:], in_=xTv[:, c, tok0:tok0 + NT]),
                   [("xT", c, tb0), ("xT", c, tb0 + 1)], [x_.name], x_.name)
            gc = self.gcol(gkind, l, c)
            sc.dve(lambda x_=x_, c=c, gc=gc: nc.vector.scalar_tensor_tensor(
                out=hT_t[:, c, col0:col0 + NT], in0=x_[:], scalar=gc, in1=rstd[:], op0=ALU.mult, op1=ALU.mult),
                r=[x_.name, rstd.name, "gains"], w=hkeys(c))

    def stage_mlp(self, l):
        nc, sc = self.nc, self.sc
        NT = 1024
        xTv = self.xT_d.rearrange("(c p) t -> p c t", p=128)
        w1v = self.inp("mlp_w1", l).rearrange("(kc p) n -> p kc n", p=128)
        w2v = self.inp("mlp_w2", l).rearrange("(m p) n -> p m n", p=128)
        ones_f = self.ones_f
        with contextlib.ExitStack() as st:
            hT = self.sb(st, "hT", [128, KC, NT], BF16)
            uT = self.sb(st, "uT", [128, 32, NT], BF16)
            w1s = [self.sb(st, "w1s", [128, KC, 256], BF16) for _ in range(2)]
            w2s = [self.sb(st, "w2s", [128, 32, 128], BF16) for _ in range(2)]
            xc = [self.sb(st, "xc", [128, NT], F32) for _ in range(3)]
            sqb = [self.sb(st, "sqb", [128, NT], F32) for _ in range(2)]
            rstd = self.sb(st, "rstd", [128, NT], F32)
            rl = [self.sb(st, "rl", [128, 512], F32) for _ in range(2)]
            xr = [self.sb(st, "xr", [128, 512], F32) for _ in range(3)]
            pss = [self.ps(st, "pss", [128, 512], F32) for _ in range(2)]
            pu = [self.ps(st, "pu", [128, 512], F32) for _ in range(3)]
            py = [self.ps(st, "py", [128, 512], F32) for _ in range(3)]
            for blk in range(S // NT):
                tok0 = blk * NT
                self.norm_block(st, (hT, 0), lambda c: [("hT", c)], tok0, "mlp", l, xc, sqb, rstd, pss)
                wi = 0
                w2i = 0
                ui = 0
                yi = 0
                for hh in range(2):
                    for j in range(16):
                        ws = w1s[wi % 2]
                        wi += 1
                        n0 = hh * 4096 + j * 256
                        sc.dma("pool", lambda ws=ws, n0=n0: nc.gpsimd.dma_start(out=ws[:], in_=w1v[:, :, n0:n0 + 256]),
                               [], [ws.name], ws.name)
                        for mm in range(2):
                            m = j * 2 + mm
                            for th in range(2):
                                p = pu[ui % 3]
                                r_ = rl[ui % 2]
                                ui += 1
                                for kc in range(KC):
                                    sc.pe(lambda p=p, ws=ws, mm=mm, kc=kc, th=th: nc.tensor.matmul(
                                        p[:], ws[:, kc, mm * 128:(mm + 1) * 128], hT[:, kc, th * 512:(th + 1) * 512],
                                        start=(kc == 0), stop=(kc == KC - 1)),
                                        r=[ws.name, ("hT", kc)], w=[p.name])
                                sc.act(lambda p=p, r_=r_: nc.scalar.activation(out=r_[:], in_=p[:], func=AF.Relu),
                                       r=[p.name], w=[r_.name])
                                sc.dve(lambda r_=r_, m=m, th=th: nc.vector.tensor_tensor(
                                    uT[:, m, th * 512:(th + 1) * 512], r_[:], r_[:], ALU.mult),
                                    r=[r_.name], w=[("uT", m, th)])
                    for n in range(KC):
                        ws = w2s[w2i % 2]
                        w2i += 1
                        sc.dma("pool", lambda ws=ws, n=n, hh=hh: nc.gpsimd.dma_start(
                            out=ws[:], in_=w2v[:, hh * 32:(hh + 1) * 32, n * 128:(n + 1) * 128]),
                            [], [ws.name], ws.name)
                        for th in range(2):
                            p = py[yi % 3]
                            x_ = xr[yi % 3]
                            yi += 1
                            tb = tok0 // 512 + th
                            sc.dma("sp", lambda x_=x_, n=n, tb=tb: nc.sync.dma_start(out=x_[:], in_=xTv[:, n, tb * 512:(tb + 1) * 512]),
                                   [("xT", n, tb)], [x_.name], x_.name)
                            for m in range(32):
                                sc.pe(lambda p=p, ws=ws, m=m, th=th: nc.tensor.matmul(
                                    p[:], ws[:, m, :], uT[:, m, th * 512:(th + 1) * 512], start=(m == 0), stop=(m == 31)),
                                    r=[ws.name, ("uT", m, th)], w=[p.name])
                            sc.dve(lambda p=p, x_=x_: nc.vector.tensor_tensor(x_[:], x_[:], p[:], ALU.add),
                                   r=[p.name, x_.name], w=[x_.name])
                            sc.dma("sp", lambda x_=x_, n=n, tb=tb: nc.sync.dma_start(out=xTv[:, n, tb * 512:(tb + 1) * 512], in_=x_[:]),
                                   [x_.name], [("xT", n, tb)], "xT")
            sc.barrier()

    def stage_dbg(self):
        nc, sc = self.nc, self.sc
        sc.dma("sp", lambda: nc.sync.dma_start(out=self.dbg_d[:, :], in_=self.xT_d[:, :]),
               [("xT", c, tb) for c in range(KC) for tb in range(4)], ["dbg"], "dbg")
        sc.barrier()

    def stage_final(self):
        nc, sc = self.nc, self.sc
        xTv = self.xT_d.rearrange("(c p) t -> p c t", p=128)
        ones_f, ident_f = self.ones_f, self.ident_f
        with contextlib.ExitStack() as st:
            xa = self.sb(st, "xa", [128, KC, 512], F32)
            sqb = [self.sb(st, "sqf", [128, 512], F32) for _ in range(2)]
            rstd = self.sb(st, "rstdf", [128, 512], F32)
            hn = [self.sb(st, "hn", [128, 512], F32) for _ in range(2)]
            yo = [self.sb(st, "yo", [128, D], F32) for _ in range(4)]
            pss = self.ps(st, "pssf", [128, 512], F32)
            pt = [self.ps(st, "ptf", [128, 512], F32) for _ in range(4)]
            k = 0
            for tb in range(4):
                for c in range(KC):
                    sc.dma("sp", lambda c=c, tb=tb: nc.sync.dma_start(out=xa[:, c, :], in_=xTv[:, c, tb * 512:(tb + 1) * 512]),
                           [("xT", c, tb)], [("xa", c)], "xa")
                    s_ = sqb[c % 2]
                    sc.act(lambda s_=s_, c=c: nc.scalar.activation(out=s_[:], in_=xa[:, c, :], func=AF.Square),
                           r=[("xa", c)], w=[s_.name])
                    sc.pe(lambda s_=s_, c=c: nc.tensor.matmul(pss[:], ones_f[:], s_[:], start=(c == 0), stop=(c == KC - 1)),
                          r=[s_.name, "ones_f"], w=[pss.name])
                if self.cfg.get("fu", 9) < 2:
                    continue
                self.rstd_from(rstd[:], pss[:], [pss.name], [rstd.name], float(D * EPS))
                if self.cfg.get("fu", 9) < 3:
                    continue
                for c in range(KC):
                    h_ = hn[c % 2]
                    gc = self.gcol("final", 0, c)
                    sc.dve(lambda h_=h_, c=c, gc=gc: nc.vector.scalar_tensor_tensor(
                        out=h_[:], in0=xa[:, c, :], scalar=gc, in1=rstd[:], op0=ALU.mult, op1=ALU.mult),
                        r=[("xa", c), rstd.name, "gains"], w=[h_.name])
                    if self.cfg.get("fu", 9) < 4:
                        continue
                    p = pt[k % 4]
                    k += 1
                    for i in range(4):
                        sc.pe(lambda p=p, h_=h_, i=i: nc.tensor.transpose(p[:, i * 128:(i + 1) * 128], h_[:, i * 128:(i + 1) * 128], ident_f[:]),
                              r=[h_.name, "ident_f"], w=[p.name])
                    if self.cfg.get("fu", 9) < 5:
                        continue
                    for i in range(4):
                        y_ = yo[i]
                        if i % 2 == 0:
                            sc.act(lambda p=p, i=i, c=c, y_=y_: nc.scalar.copy(y_[:, c * 128:(c + 1) * 128], p[:, i * 128:(i + 1) * 128]),
                                   r=[p.name], w=[y_.name])
                        else:
                            sc.dve(lambda p=p, i=i, c=c, y_=y_: nc.vector.tensor_copy(y_[:, c * 128:(c + 1) * 128], p[:, i * 128:(i + 1) * 128]),
                                   r=[p.name], w=[y_.name])
                for i in range(4):
                    if self.cfg.get("fu", 9) < 6:
                        continue
                    y_ = yo[i]
                    t0 = tb * 512 + i * 128
                    sc.dma("sp", lambda y_=y_, t0=t0: nc.sync.dma_start(out=self.y_d[t0:t0 + 128, :], in_=y_[:]),
                           [y_.name], [("y", t0)], "y")
            sc.barrier()


_CACHE = {}


def _get_nc(cfg_key, cfg):
    if cfg_key not in _CACHE:
        b = Builder(cfg)
        nc = b.build()
        _CACHE[cfg_key] = (nc, b)
    return _CACHE[cfg_key]


FULL_STAGES = [("even", 0), ("mlp", 0), ("odd", 1), ("mlp", 1), ("even", 2), ("mlp", 2), ("odd", 3), ("mlp", 3)]
INPUT_NAMES = ["norm_mix", "norm_mlp", "ev_w_in", "mla_q_norm", "mla_w_uq", "mla_kv_norm", "mla_w_ukv",
               "mlstm_conv_w", "mlstm_conv_b", "mlstm_b_i", "mlstm_b_f", "ev_w_out", "od_w_qkv", "od_w_out",
               "mlp_w1", "mlp_w2", "norm_final"]


def run(inputs, stages, dbg=False, trace=False, **kw):
    cfg = {"stages": stages, "dbg": dbg}
    cfg.update({k: v for k, v in kw.items() if k != "ncores"})
    nc, b = _get_nc(repr(sorted(cfg.items(), key=str)), cfg)
    x = np.ascontiguousarray(inputs["x"], dtype=np.float32)
    shared = {}
    for k in b.inputs:
        if k == "x":
            continue
        base, _, idx = k.rpartition("_")
        if base in Builder.PER_LAYER:
            shared[k] = np.ascontiguousarray(inputs[base][int(idx)], dtype=np.float32)
        else:
            shared[k] = np.ascontiguousarray(inputs[k], dtype=np.float32)
    ncores = kw.get("ncores", NCORES)
    zeros = {k: np.zeros_like(v) for k, v in shared.items()}
    zx = np.zeros_like(x[0])
    in_maps = []
    for c in range(ncores):
        if c % 2 == 0 or ncores < NCORES:
            m = dict(shared)
            m["x"] = np.ascontiguousarray(x[(c // 2) % 4] if ncores == NCORES else x[c % 4])
        else:
            m = dict(zeros)
            m["x"] = zx
        in_maps.append(m)
    res = run_bass_kernel_spmd(nc, in_maps, core_ids=list(range(ncores)), trace=trace)
    return res


def kernel(**inputs):
    res = run(inputs, FULL_STAGES)
    out = np.stack([np.asarray(res.results[2 * b]["y"], dtype=np.float32) for b in range(4)], axis=0)
    return out


def _rope_tables(self, st, d):
    nc, sc = self.nc, self.sc
    half = d // 2
    cos = self.sb(st, "cos", [128, S], F32)
    sin = self.sb(st, "sin", [128, S], F32)
    with contextlib.ExitStack() as t:
        pid = self.sb(t, "pid", [128, 1], I32)
        pim = self.sb(t, "pim", [128, 1], I32)
        pf = self.sb(t, "pf", [128, 1], F32)
        inv = self.sb(t, "inv", [128, 1], F32)
        sgn = self.sb(t, "sgn", [128, 1], F32)
        tpos = self.sb(t, "tpos", [128, S], F32)
        u = self.sb(t, "u", [128, S], F32)
        v = self.sb(t, "v", [128, S], F32)
        vi = self.sb(t, "vi", [128, S], I32)
        fr = self.sb(t, "fr", [128, S], F32)
        sc.pool(lambda: nc.gpsimd.iota(pid[:], [[0, 1]], base=0, channel_multiplier=1), w=[pid.name])
        sc.dve(lambda: nc.vector.tensor_single_scalar(pim[:], pid[:], half - 1, ALU.bitwise_and), r=[pid.name], w=[pim.name])
        sc.dve(lambda: nc.vector.tensor_copy(pf[:], pim[:]), r=[pim.name], w=[pf.name])
        sc.act(lambda: nc.scalar.activation(out=inv[:], in_=pf[:], func=AF.Exp, scale=-(2.0 * math.log(10000.0) / d)),
               r=[pf.name], w=[inv.name])
        sc.dve(lambda: nc.vector.tensor_single_scalar(pim[:], pid[:], half, ALU.bitwise_and), r=[pid.name, pf.name], w=[pim.name])
        sc.dve(lambda: nc.vector.tensor_copy(sgn[:], pim[:]), r=[pim.name], w=[sgn.name])
        sc.dve(lambda: nc.vector.tensor_scalar(sgn[:], sgn[:], 2.0 / half, -1.0, op0=ALU.mult, op1=ALU.add),
               r=[sgn.name], w=[sgn.name])
        sc.pool(lambda: nc.gpsimd.iota(tpos[:], [[1, S]], base=0, channel_multiplier=0, allow_small_or_imprecise_dtypes=True),
                w=[tpos.name])
        sc.dve(lambda: nc.vector.tensor_scalar(u[:], tpos[:], inv[:, 0:1], 1.0 / (2.0 * math.pi), op0=ALU.mult, op1=ALU.mult),
               r=[tpos.name, inv.name], w=[u.name])
        for dst, shift in ((sin, 0.0), (cos, 0.25)):
            sc.dve(lambda shift=shift: nc.vector.tensor_scalar_add(v[:], u[:], shift), r=[u.name], w=[v.name])
            sc.dve(lambda: nc.vector.tensor_copy(vi[:], v[:]), r=[v.name], w=[vi.name])
            sc.dve(lambda: nc.vector.tensor_copy(fr[:], vi[:]), r=[vi.name], w=[fr.name])
            sc.dve(lambda: nc.vector.tensor_sub(fr[:], v[:], fr[:]), r=[v.name, fr.name], w=[fr.name])
            sc.dve(lambda: nc.vector.tensor_single_scalar(v[:], fr[:], 0.5, ALU.is_gt), r=[fr.name], w=[v.name])
            sc.dve(lambda: nc.vector.tensor_sub(fr[:], fr[:], v[:]), r=[v.name, fr.name], w=[fr.name])
            sc.dve(lambda: nc.vector.tensor_single_scalar(v[:], fr[:], -0.5, ALU.is_lt), r=[fr.name], w=[v.name])
            sc.dve(lambda: nc.vector.tensor_add(fr[:], fr[:], v[:]), r=[v.name, fr.name], w=[fr.name])
            sc.act(lambda dst=dst: nc.scalar.activation(out=dst[:], in_=fr[:], func=AF.Sin, scale=2.0 * math.pi * (1.0 - 2e-7)),
                   r=[fr.name], w=[dst.name])
        sc.dve(lambda: nc.vector.tensor_scalar(sin[:], sin[:], sgn[:, 0:1], None, op0=ALU.mult), r=[sin.name, sgn.name], w=[sin.name])
        sc.barrier()
    return cos, sin


def _rope_apply(self, ps_ap, np_, half, cos_ap, sin_ap, out_ap, tmp, rkeys, wkeys, k):
    nc, sc = self.nc, self.sc
    xsw, t1 = tmp
    n = np_
    sc.act(lambda: nc.scalar.copy(xsw[0:half, :], ps_ap[half:n, :]), r=rkeys, w=[xsw.name])
    sc.act(lambda: nc.scalar.copy(xsw[half:n, :], ps_ap[0:half, :]), r=rkeys, w=[xsw.name])
    sc.dve(lambda: nc.vector.tensor_tensor(t1[0:n, :], ps_ap, cos_ap, ALU.mult), r=rkeys + ["ropetab"], w=[t1.name])
    sc.pool(lambda: nc.gpsimd.tensor_tensor(xsw[0:n, :], xsw[0:n, :], sin_ap, ALU.mult), r=[xsw.name, "ropetab"], w=[xsw.name])
    if k % 2 == 0:
        sc.pool(lambda: nc.gpsimd.tensor_tensor(out_ap, t1[0:n, :], xsw[0:n, :], ALU.add), r=[xsw.name, t1.name], w=wkeys)
    else:
        sc.dve(lambda: nc.vector.tensor_tensor(out_ap, t1[0:n, :], xsw[0:n, :], ALU.add), r=[xsw.name, t1.name], w=wkeys)


def _dil_masks(self, st):
    nc, sc = self.nc, self.sc
    d0s = [-384, -256, -128, 0, 128, 256, 384, 512, 640]
    masks = {d0: self.sb(st, "mask", [128, 512], BF16) for d0 in d0s}
    with contextlib.ExitStack() as t:
        di = self.sb(t, "di", [128, 512], I32)
        da = self.sb(t, "da", [128, 512], I32)
        c0 = self.sb(t, "c0", [128, 512], F32)
        c1 = self.sb(t, "c1", [128, 512], F32)
        c2 = self.sb(t, "c2", [128, 512], F32)
        for d0 in d0s:
            m = masks[d0]
            sc.pool(lambda d0=d0: nc.gpsimd.iota(di[:], [[1, 512]], base=d0, channel_multiplier=-1), r=[c0.name], w=[di.name])
            sc.dve(lambda: nc.vector.tensor_single_scalar(c1[:], di[:], 128, ALU.is_le), r=[di.name], w=[c1.name])
            sc.dve(lambda: nc.vector.tensor_single_scalar(da[:], di[:], 3, ALU.bitwise_and), r=[di.name], w=[da.name])
            sc.dve(lambda: nc.vector.tensor_single_scalar(c2[:], da[:], 0, ALU.is_equal), r=[da.name], w=[c2.name])
            sc.dve(lambda: nc.vector.tensor_single_scalar(c0[:], di[:], 512, ALU.is_le), r=[di.name], w=[c0.name])
            sc.dve(lambda: nc.vector.tensor_tensor(c2[:], c2[:], c0[:], ALU.mult), r=[c0.name, c2.name], w=[c2.name])
            sc.dve(lambda: nc.vector.tensor_tensor(c1[:], c1[:], c2[:], ALU.add), r=[c1.name, c2.name], w=[c1.name])
            sc.dve(lambda: nc.vector.tensor_single_scalar(da[:], di[:], 15, ALU.bitwise_and), r=[di.name], w=[da.name])
            sc.dve(lambda: nc.vector.tensor_single_scalar(c2[:], da[:], 0, ALU.is_equal), r=[da.name], w=[c2.name])
            sc.dve(lambda: nc.vector.tensor_tensor(c1[:], c1[:], c2[:], ALU.add), r=[c1.name, c2.name], w=[c1.name])
            sc.dve(lambda: nc.vector.tensor_single_scalar(c0[:], di[:], 0, ALU.is_ge), r=[di.name], w=[c0.name])
            sc.dve(lambda m=m: nc.vector.tensor_tensor(m[:], c1[:], c0[:], ALU.mult), r=[c0.name, c1.name], w=["masks"])
        sc.barrier()
    return masks


def _out_proj(self, w_ap, l):
    nc, sc = self.nc, self.sc
    xTv = self.xT_d.rearrange("(c p) t -> p c t", p=128)
    oTv = self.oT_d.rearrange("(c p) t -> p c t", p=128)
    wv = w_ap.rearrange("(kc p) n -> p kc n", p=128)
    with contextlib.ExitStack() as st:
        oT = self.sb(st, "oT", [128, KC, S], BF16)
        ws_ = [self.sb(st, "wo", [128, KC, 256], BF16) for _ in range(2)]
        xr = [self.sb(st, "xr", [128, 512], F32) for _ in range(3)]
        py = [self.ps(st, "py", [128, 512], F32) for _ in range(3)]
        for c in range(KC):
            sc.dma("sp", lambda c=c: nc.sync.dma_start(out=oT[:, c, :], in_=oTv[:, c, :]), [("oT_d", c)], [("oT", c)], None)
        yi = 0
        for j in range(8):
            ws = ws_[j % 2]
            sc.dma("pool", lambda ws=ws, j=j: nc.gpsimd.dma_start(out=ws[:], in_=wv[:, :, j * 256:(j + 1) * 256]), [], [ws.name], None)
            for nn in range(2):
                n = j * 2 + nn
                for tb in range(4):
                    p = py[yi % 3]
                    x_ = xr[yi % 3]
                    yi += 1
                    sc.dma("sp", lambda x_=x_, n=n, tb=tb: nc.sync.dma_start(out=x_[:], in_=xTv[:, n, tb * 512:(tb + 1) * 512]),
                           [("xT", n, tb)], [x_.name], None)
                    for kc in range(KC):
                        sc.pe(lambda p=p, ws=ws, nn=nn, kc=kc, tb=tb: nc.tensor.matmul(
                            p[:], ws[:, kc, nn * 128:(nn + 1) * 128], oT[:, kc, tb * 512:(tb + 1) * 512],
                            start=(kc == 0), stop=(kc == KC - 1)), r=[ws.name, ("oT", kc)], w=[p.name])
                    sc.dve(lambda p=p, x_=x_: nc.vector.tensor_tensor(x_[:], x_[:], p[:], ALU.add), r=[p.name, x_.name], w=[x_.name])
                    sc.dma("sp", lambda x_=x_, n=n, tb=tb: nc.sync.dma_start(out=xTv[:, n, tb * 512:(tb + 1) * 512], in_=x_[:]),
                           [x_.name], [("xT", n, tb)], None)
        sc.barrier()


def _stage_odd(self, l):
    nc, sc = self.nc, self.sc
    i = l // 2
    wqkv = self.inp("od_w_qkv", i).rearrange("(kc p) n -> p kc n", p=128)
    oTv = self.oT_d.rearrange("(c p) t -> p c t", p=128)
    scale = 128.0 ** -0.5
    ones_b = self.ones_b
    with contextlib.ExitStack() as st:
        cos, sin = self.rope_tables(st, 128)
        masks = self.dil_masks(st)
        hT = self.sb(st, "hT", [128, KC, S], BF16)
        with contextlib.ExitStack() as t:
            xc = [self.sb(t, "xc", [128, 1024], F32) for _ in range(3)]
            sqb = [self.sb(t, "sqb", [128, 1024], F32) for _ in range(2)]
            rstd = self.sb(t, "rstd", [128, 1024], F32)
            pss = [self.ps(t, "pss", [128, 512], F32) for _ in range(2)]
            for blk in range(2):
                self.norm_block(t, (hT, blk * 1024), lambda c, blk=blk: [("hT", c, blk)], blk * 1024, "mix", l, xc, sqb, rstd, pss)
            sc.barrier()
        QT = self.sb(st, "QT", [128, 2, S], BF16)
        KT = self.sb(st, "KT", [128, 2, S], BF16)
        V = self.sb(st, "V", [128, 16, 256], BF16)
        wsl = [self.sb(st, "wq", [128, KC, 256], BF16) for _ in range(3)]
        xsw = [self.sb(st, "xsw", [128, 512], F32) for _ in range(2)]
        t1 = [self.sb(st, "t1", [128, 512], F32) for _ in range(2)]
        E = [self.sb(st, "E", [128, 512], BF16) for _ in range(3)]
        rden = self.sb(st, "rden", [128, 512], F32)
        ot = [self.sb(st, "ot", [128, 512], BF16) for _ in range(2)]
        pq = [self.ps(st, "pq", [128, 512], F32) for _ in range(2)]
        pv = self.ps(st, "pv", [128, 512], F32)
        pS = [self.ps(st, "pS", [128, 512], F32) for _ in range(2)]
        po = self.ps(st, "po", [128, 512], F32)
        pd = self.ps(st, "pd", [128, 512], F32)
        hkeys_all = lambda kc: [("hT", kc, 0), ("hT", kc, 1)]
        wi = 0
        qi = 0
        ei = 0
        oi = 0
        for g in range(8):
            wts = []
            for part in range(3):
                ws = wsl[wi % 3]
                wi += 1
                c0 = part * 2048 + g * 256
                sc.dma("pool", lambda ws=ws, c0=c0: nc.gpsimd.dma_start(out=ws[:], in_=wqkv[:, :, c0:c0 + 256]), [], [ws.name], None)
                wts.append(ws)
            for part, dst, dname in ((0, QT, "QT"), (1, KT, "KT")):
                ws = wts[part]
                for hh in range(2):
                    for tb in range(4):
                        p = pq[qi % 2]
                        tmp = (xsw[qi % 2], t1[qi % 2])
                        for kc in range(KC):
                            sc.pe(lambda p=p, ws=ws, hh=hh, kc=kc, tb=tb: nc.tensor.matmul(
                                p[:], ws[:, kc, hh * 128:(hh + 1) * 128], hT[:, kc, tb * 512:(tb + 1) * 512],
                                start=(kc == 0), stop=(kc == KC - 1)), r=[ws.name, ("hT", kc, tb // 2)], w=[p.name])
                        self.rope_apply(p[:], 128, 64, cos[:, tb * 512:(tb + 1) * 512], sin[:, tb * 512:(tb + 1) * 512],
                                        dst[:, hh, tb * 512:(tb + 1) * 512], tmp, [p.name], [(dname, hh, tb)], qi)
                        qi += 1
            ws = wts[2]
            for tt in range(16):
                for kc in range(KC):
                    sc.pe(lambda ws=ws, kc=kc, tt=tt: nc.tensor.matmul(
                        pv[:, 0:256], hT[:, kc, tt * 128:(tt + 1) * 128], ws[:, kc, :], start=(kc == 0), stop=(kc == KC - 1)),
                        r=[ws.name, ("hT", kc, tt // 8)], w=[pv.name])
                sc.act(lambda tt=tt: nc.scalar.copy(V[:, tt, :], pv[:, 0:256]), r=[pv.name], w=[("V", tt)])
            for hh in range(2):
                head = g * 2 + hh
                for Qs in range(4):
                    nkb = 4 * Qs + 4
                    for kb in range(nkb):
                        d0 = 512 * Qs - 128 * kb
                        m = masks[min(d0, 640)]
                        p = pS[ei % 2]
                        e_ = E[ei % 3]
                        sc.pe(lambda p=p, hh=hh, kb=kb, Qs=Qs: nc.tensor.matmul(
                            p[:], KT[:, hh, kb * 128:(kb + 1) * 128], QT[:, hh, Qs * 512:(Qs + 1) * 512], start=True, stop=True),
                            r=[("KT", hh, kb // 4), ("QT", hh, Qs)], w=[p.name])
                        sc.act(lambda p=p, e_=e_: nc.scalar.activation(out=e_[:], in_=p[:], func=AF.Exp, scale=scale),
                               r=[p.name], w=[e_.name])
                        if ei % 2 == 0:
                            sc.pool(lambda e_=e_, m=m: nc.gpsimd.tensor_tensor(e_[:], e_[:], m[:], ALU.mult), r=[e_.name, "masks"], w=[e_.name])
                        else:
                            sc.dve(lambda e_=e_, m=m: nc.vector.tensor_tensor(e_[:], e_[:], m[:], ALU.mult), r=[e_.name, "masks"], w=[e_.name])
                        ei += 1
                        sc.pe(lambda e_=e_, hh=hh, kb=kb, nkb=nkb: nc.tensor.matmul(
                            po[:], V[:, kb, hh * 128:(hh + 1) * 128], e_[:], start=(kb == 0), stop=(kb == nkb - 1)),
                            r=[e_.name, ("V", kb)], w=[po.name])
                        sc.pe(lambda e_=e_, kb=kb, nkb=nkb: nc.tensor.matmul(
                            pd[:], ones_b[:], e_[:], start=(kb == 0), stop=(kb == nkb - 1)),
                            r=[e_.name, "ones_b"], w=[pd.name])
                    o_ = ot[oi % 2]
                    oi += 1
                    sc.dve(lambda: nc.vector.reciprocal(rden[:], pd[:]), r=[pd.name], w=[rden.name])
                    sc.dve(lambda o_=o_: nc.vector.tensor_tensor(o_[:], po[:], rden[:], ALU.mult), r=[po.name, rden.name], w=[o_.name])
                    sc.dma("sp", lambda o_=o_, head=head, Qs=Qs: nc.sync.dma_start(out=oTv[:, head, Qs * 512:(Qs + 1) * 512], in_=o_[:]),
                           [o_.name], [("oT_d", head)], None)
        sc.barrier()
    self.out_proj(self.inp("od_w_out", i), l)


Builder.rope_tables = _rope_tables
Builder.rope_apply = _rope_apply
Builder.dil_masks = _dil_masks
Builder.out_proj = _out_proj
Builder.stage_odd = _stage_odd


def _norm_tmp(self, t):
    xc = [self.sb(t, "xc", [128, 1024], F32) for _ in range(3)]
    sqb = [self.sb(t, "sqb", [128, 1024], F32) for _ in range(2)]
    rstd = self.sb(t, "rstd", [128, 1024], F32)
    pss = [self.ps(t, "pss", [128, 512], F32) for _ in range(2)]
    return xc, sqb, rstd, pss


def _mla(self, l):
    nc, sc = self.nc, self.sc
    i = l // 2
    w_in = self.inp("ev_w_in", i).rearrange("(kc p) n -> p kc n", p=128)
    wuq_d = self.inp("mla_w_uq", i).rearrange("(c p) n -> p c n", p=128)
    wukv_d = self.inp("mla_w_ukv", i).rearrange("(c p) n -> p c n", p=128)
    oTv = self.oT_d.rearrange("(c p) t -> p c t", p=128)
    ones_f, ones_b = self.ones_f, self.ones_b
    scale = 192.0 ** -0.5
    with contextlib.ExitStack() as st:
        cos, sin = self.rope_tables(st, 64)
        cqn = self.sb(st, "cqn", [128, 4, S], BF16)
        ckvn = self.sb(st, "ckvn", [128, 4, S], BF16)
        krope = self.sb(st, "krope", [128, S], BF16)
        with contextlib.ExitStack() as t:
            hT = self.sb(t, "hT", [128, KC, S], BF16)
            wsl = [self.sb(t, "wl", [128, KC, 256], BF16) for _ in range(2)]
            lat = self.sb(t, "lat", [128, 4, 512], F32)
            sq = [self.sb(t, "sql", [128, 512], F32) for _ in range(2)]
            rs = self.sb(t, "rsl", [128, 512], F32)
            xsw = [self.sb(t, "xsw", [128, 512], F32) for _ in range(2)]
            t1 = [self.sb(t, "t1", [128, 512], F32) for _ in range(2)]
            pl = [self.ps(t, "pl", [128, 512], F32) for _ in range(3)]
            pn = self.ps(t, "pn", [128, 512], F32)
            with contextlib.ExitStack() as t2:
                xc, sqb, rstd, pss = self.norm_tmp(t2)
                for blk in range(2):
                    self.norm_block(t2, (hT, blk * 1024), lambda c, blk=blk: [("hT", c, blk)], blk * 1024, "mix", l, xc, sqb, rstd, pss)
                sc.barrier()
            wi = 0
            pi = 0
            for which, dst, gk in ((0, cqn, "qn"), (1, ckvn, "kvn")):
                wts = []
                for half in range(2):
                    ws = wsl[wi % 2]
                    wi += 1
                    c0 = which * 512 + half * 256
                    sc.dma("pool", lambda ws=ws, c0=c0: nc.gpsimd.dma_start(out=ws[:], in_=w_in[:, :, c0:c0 + 256]), [], [ws.name], None)
                    wts.append(ws)
                for tb in range(4):
                    for c in range(4):
                        ws = wts[c // 2]
                        p = pl[pi % 3]
                        pi += 1
                        for kc in range(KC):
                            sc.pe(lambda p=p, ws=ws, c=c, kc=kc, tb=tb: nc.tensor.matmul(
                                p[:], ws[:, kc, (c % 2) * 128:(c % 2 + 1) * 128], hT[:, kc, tb * 512:(tb + 1) * 512],
                                start=(kc == 0), stop=(kc == KC - 1)), r=[ws.name, ("hT", kc, tb // 2)], w=[p.name])
                        s_ = sq[c % 2]
                        sc.act(lambda p=p, c=c: nc.scalar.copy(lat[:, c, :], p[:]), r=[p.name], w=[("lat", c)])
                        sc.act(lambda s_=s_, c=c: nc.scalar.activation(out=s_[:], in_=lat[:, c, :], func=AF.Square), r=[("lat", c)], w=[s_.name])
                        sc.pe(lambda s_=s_, c=c: nc.tensor.matmul(pn[:], ones_f[:], s_[:], start=(c == 0), stop=(c == 3)),
                              r=[s_.name, "ones_f"], w=[pn.name])
                    self.rstd_from(rs[:], pn[:], [pn.name], [rs.name], float(512 * EPS))
                    for c in range(4):
                        gc = self.gcol(gk, i, c)
                        sc.dve(lambda c=c, gc=gc, dst=dst, tb=tb: nc.vector.scalar_tensor_tensor(
                            out=dst[:, c, tb * 512:(tb + 1) * 512], in0=lat[:, c, :], scalar=gc, in1=rs[:], op0=ALU.mult, op1=ALU.mult),
                            r=[("lat", c), rs.name, "gains"], w=[(dst.name, tb)])
            ws = wsl[wi % 2]
            wi += 1
            sc.dma("pool", lambda ws=ws: nc.gpsimd.dma_start(out=ws[:, :, 0:64], in_=w_in[:, :, 1024:1088]), [], [ws.name], None)
            for tb in range(4):
                p = pl[pi % 3]
                pi += 1
                for kc in range(KC):
                    sc.pe(lambda p=p, ws=ws, kc=kc, tb=tb: nc.tensor.matmul(
                        p[0:64, :], ws[:, kc, 0:64], hT[:, kc, tb * 512:(tb + 1) * 512], start=(kc == 0), stop=(kc == KC - 1)),
                        r=[ws.name, ("hT", kc, tb // 2)], w=[p.name])
                self.rope_apply(p[0:64, :], 64, 32, cos[0:64, tb * 512:(tb + 1) * 512], sin[0:64, tb * 512:(tb + 1) * 512],
                                krope[0:64, tb * 512:(tb + 1) * 512], (xsw[tb % 2], t1[tb % 2]), [p.name], [("krope", tb)], tb)
            sc.barrier()
        wuq = self.sb(st, "wuq", [128, 4, 1536], BF16)
        wukv = self.sb(st, "wukv", [128, 4, 2048], BF16)
        sc.dma("pool", lambda: nc.gpsimd.dma_start(out=wuq[:], in_=wuq_d), [], [wuq.name], None)
        sc.dma("pool", lambda: nc.gpsimd.dma_start(out=wukv[:], in_=wukv_d), [], [wukv.name], None)
        QN = self.sb(st, "QN", [128, S], BF16)
        QR = self.sb(st, "QR", [128, S], BF16)
        KN = self.sb(st, "KN", [128, S], BF16)
        VM = self.sb(st, "VM", [128, 16, 128], BF16)
        cm = {d0: self.sb(st, "cm", [128, 512], BF16) for d0 in (0, -128, -256, -384)}
        xsw = [self.sb(st, "xsw", [128, 512], F32) for _ in range(2)]
        t1 = [self.sb(st, "t1", [128, 512], F32) for _ in range(2)]
        E = [self.sb(st, "E", [128, 512], BF16) for _ in range(3)]
        rden = self.sb(st, "rden", [128, 512], F32)
        ot = [self.sb(st, "ot", [128, 512], BF16) for _ in range(2)]
        di = self.sb(st, "di", [128, 512], I32)
        pq = [self.ps(st, "pq", [128, 512], F32) for _ in range(2)]
        pv = self.ps(st, "pv", [128, 512], F32)
        pS = [self.ps(st, "pS", [128, 512], F32) for _ in range(2)]
        po = self.ps(st, "po", [128, 512], F32)
        pd = self.ps(st, "pd", [128, 512], F32)
        for d0, m in cm.items():
            sc.pool(lambda d0=d0: nc.gpsimd.iota(di[:], [[1, 512]], base=d0, channel_multiplier=-1), r=["cmtmp"], w=[di.name])
            sc.dve(lambda m=m: nc.vector.tensor_single_scalar(m[:], di[:], 0, ALU.is_ge), r=[di.name], w=["cm", "cmtmp"])
        qi = 0
        ei = 0
        oi = 0
        for h in range(8):
            for tb in range(4):
                for kind in range(3):
                    p = pq[qi % 2]
                    if kind == 0:
                        wt, c0, np_ = wuq, h * 192, 128
                        src, sname = cqn, cqn.name
                    elif kind == 1:
                        wt, c0, np_ = wukv, h * 256, 128
                        src, sname = ckvn, ckvn.name
                    else:
                        wt, c0, np_ = wuq, h * 192 + 128, 64
                        src, sname = cqn, cqn.name
                    for c in range(4):
                        sc.pe(lambda p=p, wt=wt, c0=c0, np_=np_, c=c, tb=tb, src=src: nc.tensor.matmul(
                            p[0:np_, :], wt[:, c, c0:c0 + np_], src[:, c, tb * 512:(tb + 1) * 512], start=(c == 0), stop=(c == 3)),
                            r=[wt.name, (sname, tb)], w=[p.name])
                    if kind == 0:
                        sc.act(lambda p=p, tb=tb: nc.scalar.copy(QN[:, tb * 512:(tb + 1) * 512], p[:]), r=[p.name], w=[("QN", tb)])
                    elif kind == 1:
                        sc.act(lambda p=p, tb=tb: nc.scalar.copy(KN[:, tb * 512:(tb + 1) * 512], p[:]), r=[p.name], w=[("KN", tb)])
                    else:
                        self.rope_apply(p[0:64, :], 64, 32, cos[0:64, tb * 512:(tb + 1) * 512], sin[0:64, tb * 512:(tb + 1) * 512],
                                        QR[0:64, tb * 512:(tb + 1) * 512], (xsw[qi % 2], t1[qi % 2]), [p.name], [("QR", tb)], qi)
                    qi += 1
            for tt in range(16):
                for c in range(4):
                    sc.pe(lambda c=c, tt=tt, h=h: nc.tensor.matmul(
                        pv[:, 0:128], ckvn[:, c, tt * 128:(tt + 1) * 128], wukv[:, c, h * 256 + 128:h * 256 + 256],
                        start=(c == 0), stop=(c == 3)), r=[wukv.name, (ckvn.name, tt // 4)], w=[pv.name])
                sc.act(lambda tt=tt: nc.scalar.copy(VM[:, tt, :], pv[:, 0:128]), r=[pv.name], w=[("VM", tt)])
            for Qs in range(4):
                nkb = 4 * Qs + 4
                for kb in range(nkb):
                    d0 = 512 * Qs - 128 * kb
                    p = pS[ei % 2]
                    e_ = E[ei % 3]
                    sc.pe(lambda p=p, kb=kb, Qs=Qs: nc.tensor.matmul(
                        p[:], KN[:, kb * 128:(kb + 1) * 128], QN[:, Qs * 512:(Qs + 1) * 512], start=True, stop=False),
                        r=[("KN", kb // 4), ("QN", Qs)], w=[p.name])
                    sc.pe(lambda p=p, kb=kb, Qs=Qs: nc.tensor.matmul(
                        p[:], krope[0:64, kb * 128:(kb + 1) * 128], QR[0:64, Qs * 512:(Qs + 1) * 512], start=False, stop=True),
                        r=[("krope", kb // 4), ("QR", Qs)], w=[p.name])
                    sc.act(lambda p=p, e_=e_: nc.scalar.activation(out=e_[:], in_=p[:], func=AF.Exp, scale=scale), r=[p.name], w=[e_.name])
                    if d0 <= 0:
                        m = cm[d0]
                        if ei % 2 == 0:
                            sc.pool(lambda e_=e_, m=m: nc.gpsimd.tensor_tensor(e_[:], e_[:], m[:], ALU.mult), r=[e_.name, "cm"], w=[e_.name])
                        else:
                            sc.dve(lambda e_=e_, m=m: nc.vector.tensor_tensor(e_[:], e_[:], m[:], ALU.mult), r=[e_.name, "cm"], w=[e_.name])
                    ei += 1
                    sc.pe(lambda e_=e_, kb=kb, nkb=nkb: nc.tensor.matmul(
                        po[:], VM[:, kb, :], e_[:], start=(kb == 0), stop=(kb == nkb - 1)), r=[e_.name, ("VM", kb)], w=[po.name])
                    sc.pe(lambda e_=e_, kb=kb, nkb=nkb: nc.tensor.matmul(
                        pd[:], ones_b[:], e_[:], start=(kb == 0), stop=(kb == nkb - 1)), r=[e_.name, "ones_b"], w=[pd.name])
                o_ = ot[oi % 2]
                oi += 1
                sc.dve(lambda: nc.vector.reciprocal(rden[:], pd[:]), r=[pd.name], w=[rden.name])
                sc.dve(lambda o_=o_: nc.vector.tensor_tensor(o_[:], po[:], rden[:], ALU.mult), r=[po.name, rden.name], w=[o_.name])
                sc.dma("sp", lambda o_=o_, h=h, Qs=Qs: nc.sync.dma_start(out=oTv[:, h, Qs * 512:(Qs + 1) * 512], in_=o_[:]),
                       [o_.name], [("oT_d", h)], None)
        sc.barrier()


Builder.norm_tmp = _norm_tmp
Builder.mla = _mla


def _mlstm(self, l):
    nc, sc = self.nc, self.sc
    i = l // 2
    w_in = self.inp("ev_w_in", i).rearrange("(kc p) n -> p kc n", p=128)
    oTv = self.oT_d.rearrange("(c p) t -> p c t", p=128)
    ogv = self.og_d.rearrange("(c p) t -> p c t", p=128)
    ones_f, ones_b, ident_f, ident_b = self.ones_f, self.ones_b, self.ident_f, self.ident_b
    NCH = 16
    with contextlib.ExitStack() as st:
        MQ = self.sb(st, "MQ", [128, 4, S], BF16)
        MK = self.sb(st, "MK", [128, 4, S], BF16)
        VL = self.sb(st, "VL", [128, 16, 1024], BF16)
        G1 = self.sb(st, "G1", [4, S], F32)
        G2 = self.sb(st, "G2", [4, S], F32)
        G3 = self.sb(st, "G3", [4, S], F32)
        bi = self.sb(st, "bi", [4, 1], F32)
        nbf = self.sb(st, "nbf", [4, 1], F32)
        sc.dma("sp", lambda: nc.sync.dma_start(out=bi[:], in_=self.inp("mlstm_b_i")[i].rearrange("(h o) -> h o", o=1)), [], [bi.name], None)
        sc.dma("sp", lambda: nc.sync.dma_start(out=nbf[:], in_=self.inp("mlstm_b_f")[i].rearrange("(h o) -> h o", o=1)), [], [nbf.name], None)
        sc.dve(lambda: nc.vector.tensor_scalar(nbf[:], nbf[:], -1.0, None, op0=ALU.mult), r=[nbf.name], w=[nbf.name])
        with contextlib.ExitStack() as t:
            hT = self.sb(t, "hT", [128, KC, 1024], BF16)
            wsl = [self.sb(t, "wm", [128, KC, 256], BF16) for _ in range(2)]
            xpad = [self.sb(t, "xpad", [128, 3 + 1024], F32) for _ in range(2)]
            acc = [self.sb(t, "acc", [128, 1024], F32) for _ in range(2)]
            carry = self.sb(t, "carry", [128, 8, 3], F32)
            ogt = [self.sb(t, "ogt", [128, 512], BF16) for _ in range(2)]
            pp = [self.ps(t, "pp", [128, 512], F32) for _ in range(3)]
            pv = self.ps(t, "pvl", [128, 512], F32)
            pg = self.ps(t, "pg", [128, 512], F32)
            xc, sqb, rstd, pss = self.norm_tmp(t)
            sc.pool(lambda: nc.gpsimd.memset(carry[:], 0.0), w=[carry.name])
            wi = 0
            pi = 0
            oi = 0
            for blk in range(2):
                tok0 = blk * 1024
                self.norm_block(t, (hT, 0), lambda c: [("hT", c)], tok0, "mix", l, xc, sqb, rstd, pss)
                hk = lambda kc: ("hT", kc)
                for j in range(4):
                    ws = wsl[wi % 2]
                    wi += 1
                    c0 = 1088 + j * 256
                    sc.dma("pool", lambda ws=ws, c0=c0: nc.gpsimd.dma_start(out=ws[:], in_=w_in[:, :, c0:c0 + 256]), [], [ws.name], None)
                    for cc in range(2):
                        ch = j * 2 + cc
                        xp = xpad[ch % 2]
                        ac = acc[ch % 2]
                        sc.dve(lambda xp=xp, ch=ch: nc.vector.tensor_copy(xp[:, 0:3], carry[:, ch, :]), r=[carry.name], w=[xp.name])
                        for th in range(2):
                            p = pp[pi % 3]
                            pi += 1
                            for kc in range(KC):
                                sc.pe(lambda p=p, ws=ws, cc=cc, kc=kc, th=th: nc.tensor.matmul(
                                    p[:], ws[:, kc, cc * 128:(cc + 1) * 128], hT[:, kc, th * 512:(th + 1) * 512],
                                    start=(kc == 0), stop=(kc == KC - 1)), r=[ws.name, hk(kc)], w=[p.name])
                            sc.act(lambda p=p, xp=xp, th=th: nc.scalar.copy(xp[:, 3 + th * 512:3 + (th + 1) * 512], p[:]), r=[p.name], w=[xp.name])
                        sc.dve(lambda xp=xp, ch=ch: nc.vector.tensor_copy(carry[:, ch, :], xp[:, 1024:1027]), r=[xp.name], w=[carry.name])
                        eng = sc.dve
                        e_ = nc.vector
                        for jj in range(4):
                            wcol = self.gcol("convw", i, jj * 8 + ch)
                            if jj == 0:
                                eng(lambda e_=e_, ac=ac, xp=xp, wcol=wcol: e_.tensor_scalar(ac[:], xp[:, 0:1024], wcol, None, op0=ALU.mult),
                                    r=[xp.name, "gains"], w=[ac.name])
                            else:
                                eng(lambda e_=e_, ac=ac, xp=xp, wcol=wcol, jj=jj: e_.scalar_tensor_tensor(
                                    out=ac[:], in0=xp[:, jj:jj + 1024], scalar=wcol, in1=ac[:], op0=ALU.mult, op1=ALU.add),
                                    r=[xp.name, ac.name, "gains"], w=[ac.name])
                        bcol = self.gcol("convb", i, ch)
                        sc.act(lambda ac=ac, bcol=bcol: nc.scalar.activation(out=ac[:], in_=ac[:], func=AF.Silu, bias=bcol, scale=1.0),
                               r=[ac.name, "gains"], w=[ac.name])
                        if ch < 4:
                            eng(lambda e_=e_, ac=ac, ch=ch, tok0=tok0: e_.tensor_copy(MQ[:, ch, tok0:tok0 + 1024], ac[:]),
                                r=[ac.name], w=[("MQ", ch, blk)])
                        else:
                            eng(lambda e_=e_, ac=ac, ch=ch, tok0=tok0: e_.tensor_scalar(MK[:, ch - 4, tok0:tok0 + 1024], ac[:], 128.0 ** -0.5, None, op0=ALU.mult),
                                r=[ac.name], w=[("MK", ch - 4, blk)])
                for h in range(4):
                    ws = wsl[wi % 2]
                    wi += 1
                    c0 = 2112 + h * 256
                    sc.dma("pool", lambda ws=ws, c0=c0: nc.gpsimd.dma_start(out=ws[:], in_=w_in[:, :, c0:c0 + 256]), [], [ws.name], None)
                    for tt in range(8):
                        for kc in range(KC):
                            sc.pe(lambda ws=ws, kc=kc, tt=tt: nc.tensor.matmul(
                                pv[:, 0:256], hT[:, kc, tt * 128:(tt + 1) * 128], ws[:, kc, :], start=(kc == 0), stop=(kc == KC - 1)),
                                r=[ws.name, hk(kc)], w=[pv.name])
                        sc.act(lambda tt=tt, h=h, blk=blk: nc.scalar.copy(VL[:, blk * 8 + tt, h * 256:(h + 1) * 256], pv[:, 0:256]),
                               r=[pv.name], w=[("VL", blk * 8 + tt, h)])
                ws = wsl[wi % 2]
                wi += 1
                sc.dma("pool", lambda ws=ws: nc.gpsimd.dma_start(out=ws[:, :, 0:8], in_=w_in[:, :, 3136:3144]), [], [ws.name], None)
                for gi, G in ((0, G1), (1, G2)):
                    for th in range(2):
                        for kc in range(KC):
                            sc.pe(lambda ws=ws, kc=kc, th=th, gi=gi: nc.tensor.matmul(
                                pg[0:4, :], ws[:, kc, gi * 4:gi * 4 + 4], hT[:, kc, th * 512:(th + 1) * 512],
                                start=(kc == 0), stop=(kc == KC - 1)), r=[ws.name, hk(kc)], w=[pg.name])
                        sc.act(lambda G=G, th=th, tok0=tok0: nc.scalar.copy(G[:, tok0 + th * 512:tok0 + (th + 1) * 512], pg[0:4, :]),
                               r=[pg.name], w=[G.name])
                for j in range(4):
                    ws = wsl[wi % 2]
                    wi += 1
                    c0 = 3144 + j * 256
                    sc.dma("pool", lambda ws=ws, c0=c0: nc.gpsimd.dma_start(out=ws[:], in_=w_in[:, :, c0:c0 + 256]), [], [ws.name], None)
                    for cc in range(2):
                        ch = j * 2 + cc
                        for th in range(2):
                            p = pp[pi % 3]
                            pi += 1
                            og_ = ogt[oi % 2]
                            oi += 1
                            for kc in range(KC):
                                sc.pe(lambda p=p, ws=ws, cc=cc, kc=kc, th=th: nc.tensor.matmul(
                                    p[:], ws[:, kc, cc * 128:(cc + 1) * 128], hT[:, kc, th * 512:(th + 1) * 512],
                                    start=(kc == 0), stop=(kc == KC - 1)), r=[ws.name, hk(kc)], w=[p.name])
                            sc.act(lambda p=p, og_=og_: nc.scalar.activation(out=og_[:], in_=p[:], func=AF.Sigmoid), r=[p.name], w=[og_.name])
                            t0 = tok0 + th * 512
                            sc.dma("sp", lambda og_=og_, ch=ch, t0=t0: nc.sync.dma_start(out=ogv[:, ch, t0:t0 + 512], in_=og_[:]),
                                   [og_.name], [("og_d", t0 // 512)], None)
            sc.barrier()
        zer = self.sb(st, "zer", [4, 128], F32)
        amax = self.sb(st, "amax", [4, NCH], F32)
        bL = self.sb(st, "bL", [4, NCH], F32)
        mseq = self.sb(st, "mseq", [4, NCH], F32)
        mprev = self.sb(st, "mprev", [4, NCH], F32)
        nMref = self.sb(st, "nMref", [4, NCH], F32)
        dec = self.sb(st, "dec", [4, NCH], F32)
        X4 = self.sb(st, "X4", [4, 4, NCH], F32)
        SEL = self.sb(st, "SEL", [4, 4, 128], F32)
        dec_rep = self.sb(st, "dec_rep", [128, 64], F32)
        u_tm = self.sb(st, "u_tm", [128, 64], F32)
        tri = self.sb(st, "tri", [128, 128], F32)
        Cst = self.sb(st, "Cst", [128, 4, 257], F32)
        Cd = self.sb(st, "Cd", [128, 4, 256], BF16)
        ndf = self.sb(st, "ndf", [128, 4], F32)
        ndr = self.sb(st, "ndr", [128, 4, 128], BF16)
        WmT = [self.sb(st, "WmT", [128, 128], BF16) for _ in range(3)]
        ku = [self.sb(st, "ku", [128, 128], BF16) for _ in range(3)]
        flo = [self.sb(st, "flo", [128, 128], F32) for _ in range(2)]
        dm = [self.sb(st, "dm", [128, 128], F32) for _ in range(2)]
        tmpn = [self.sb(st, "tmpn", [128, 128], F32) for _ in range(3)]
        ogc = [self.sb(st, "ogc", [128, 8, 128], BF16) for _ in range(2)]
        hout = [self.sb(st, "hout", [128, 8, 128], BF16) for _ in range(2)]
        pA = [self.ps(st, "pA", [128, 512], F32) for _ in range(2)]
        pB = [self.ps(st, "pB", [128, 512], F32) for _ in range(2)]
        pC = [self.ps(st, "pC", [128, 512], F32) for _ in range(2)]
        pT = self.ps(st, "pT", [128, 1024], BF16)
        pR = self.ps(st, "pR", [128, 512], F32)
        g1, g2, g3 = G1.name, G2.name, G3.name
        sc.pool(lambda: nc.gpsimd.memset(zer[:], 0.0), w=[zer.name])
        sc.pool(lambda: nc.gpsimd.memset(Cst[:], 0.0), w=["Cst"])
        sc.pool(lambda: nc.gpsimd.memset(Cd[:], 0.0), w=["Cd"])
        sc.pool(lambda: nc.gpsimd.memset(ndr[:], 0.0), w=["ndr"])
        sc.pool(lambda: nc.gpsimd.affine_select(tri[:], ones_f[:], [[1, 128]], ALU.is_ge, 0.0, base=0, channel_multiplier=-1),
                r=["ones_f"], w=[tri.name])
        sc.dve(lambda: nc.vector.tensor_scalar(G1[:], G1[:], bi[:, 0:1], None, op0=ALU.add), r=[g1, bi.name], w=[g1])
        sc.act(lambda: nc.scalar.activation(out=G2[:], in_=G2[:], func=AF.Exp, bias=nbf[:, 0:1], scale=-1.0), r=[g2, nbf.name], w=[g2])
        sc.dve(lambda: nc.vector.tensor_scalar(G2[:], G2[:], 1.0, None, op0=ALU.add), r=[g2], w=[g2])
        sc.act(lambda: nc.scalar.activation(out=G2[:], in_=G2[:], func=AF.Ln), r=[g2], w=[g2])
        sc.dve(lambda: nc.vector.tensor_scalar(G2[:], G2[:], -1.0, None, op0=ALU.mult), r=[g2], w=[g2])
        for c in range(NCH):
            sc.dve(lambda c=c: nc.vector.tensor_tensor_scan(G3[:, c * 128:(c + 1) * 128], zer[:], G2[:, c * 128:(c + 1) * 128], 0.0, ALU.add, ALU.add),
                   r=[g2, zer.name], w=[g3])
        sc.dve(lambda: nc.vector.tensor_sub(G1[:], G1[:], G3[:]), r=[g1, g3], w=[g1])
        sc.dve(lambda: nc.vector.tensor_reduce(out=amax[:], in_=G1[:].rearrange("p (c t) -> p c t", t=128), axis=AX.X, op=ALU.max),
               r=[g1], w=[amax.name])
        sc.dve(lambda: nc.vector.tensor_copy(bL[:], G3[:].rearrange("p (c t) -> p c t", t=128)[:, :, 127]), r=[g3], w=[bL.name])
        sc.dve(lambda: nc.vector.tensor_tensor_scan(mseq[:], amax[:], bL[:], 0.0, ALU.max, ALU.add), r=[amax.name, bL.name], w=[mseq.name])
        sc.dve(lambda: nc.vector.memset(mprev[:], 0.0), w=[mprev.name])
        sc.dve(lambda: nc.vector.tensor_copy(mprev[:, 1:NCH], mseq[:, 0:NCH - 1]), r=[mseq.name, mprev.name], w=[mprev.name])
        sc.dve(lambda: nc.vector.tensor_tensor(nMref[:], mprev[:], amax[:], ALU.max), r=[mprev.name, amax.name], w=[nMref.name])
        sc.dve(lambda: nc.vector.tensor_sub(dec[:], mprev[:], nMref[:]), r=[mprev.name, nMref.name], w=[dec.name])
        sc.act(lambda: nc.scalar.activation(out=dec[:], in_=dec[:], func=AF.Exp), r=[dec.name], w=[dec.name])
        sc.dve(lambda: nc.vector.tensor_scalar(nMref[:], nMref[:], -1.0, None, op0=ALU.mult), r=[nMref.name, dec.name], w=[nMref.name])
        for c in range(NCH):
            sc.act(lambda c=c: nc.scalar.activation(out=G1[:, c * 128:(c + 1) * 128], in_=G1[:, c * 128:(c + 1) * 128], func=AF.Exp,
                                                    bias=nMref[:, c:c + 1], scale=1.0), r=[g1, nMref.name, amax.name], w=[g1])
            sc.act(lambda c=c: nc.scalar.activation(out=G3[:, c * 128:(c + 1) * 128], in_=G3[:, c * 128:(c + 1) * 128], func=AF.Exp,
                                                    bias=nMref[:, c:c + 1], scale=-1.0), r=[g3, nMref.name, bL.name], w=[g3])
        for h in range(4):
            sc.dve(lambda h=h: nc.vector.tensor_scalar(X4[:, h, :], dec[:], ident_f[0:4, h:h + 1], None, op0=ALU.mult),
                   r=[dec.name, "ident_f"], w=[X4.name])
            sc.dve(lambda h=h: nc.vector.tensor_scalar(SEL[:, h, :], ones_f[0:4, :], ident_f[0:4, h:h + 1], None, op0=ALU.mult),
                   r=["ones_f", "ident_f"], w=[SEL.name])
        sc.pe(lambda: nc.tensor.matmul(pR[:, 0:64], ones_f[0:4, :], X4[:].rearrange("p a b -> p (a b)"), start=True, stop=True),
              r=[X4.name, "ones_f"], w=[pR.name])
        sc.dve(lambda: nc.vector.tensor_copy(dec_rep[:], pR[:, 0:64]), r=[pR.name], w=[dec_rep.name])
        for c in range(NCH):
            sc.pe(lambda c=c: nc.tensor.matmul(pR[:, 64 + c * 4:64 + c * 4 + 4], G1[:, c * 128:(c + 1) * 128], ident_f[0:4, 0:4], start=True, stop=True),
                  r=[g1, "ident_f"], w=[pR.name])
        sc.dve(lambda: nc.vector.tensor_copy(u_tm[:], pR[:, 64:128]), r=[pR.name], w=[u_tm.name])
        k = 0
        for c in range(NCH):
            cs = slice(c * 128, (c + 1) * 128)
            og_ = ogc[c % 2]
            ho = hout[c % 2]
            sc.dma("sp", lambda og_=og_, cs=cs: nc.sync.dma_start(out=og_[:], in_=ogv[:, :, cs]), [("og_d", c // 4)], [og_.name], None)
            for h in range(4):
                a_ = pA[k % 2]
                b_ = pB[k % 2]
                c_ = pC[k % 2]
                wm = WmT[k % 3]
                ku_ = ku[k % 3]
                fl = flo[k % 2]
                dm_ = dm[k % 2]
                ucol = u_tm[:, c * 4 + h:c * 4 + h + 1]
                blk = c // 8
                sc.pe(lambda a_=a_, h=h, cs=cs: nc.tensor.matmul(a_[:, 0:128], MK[:, h, cs], MQ[:, h, cs], start=True, stop=True),
                      r=[("MK", h, blk), ("MQ", h, blk)], w=[a_.name])
                sc.dve(lambda a_=a_, wm=wm, ucol=ucol: nc.vector.scalar_tensor_tensor(
                    out=wm[:], in0=a_[:, 0:128], scalar=ucol, in1=tri[:], op0=ALU.mult, op1=ALU.mult),
                    r=[a_.name, u_tm.name, tri.name], w=[wm.name])
                for j in range(2):
                    sc.pe(lambda b_=b_, wm=wm, h=h, j=j, c=c: nc.tensor.matmul(
                        b_[:, j * 128:(j + 1) * 128], VL[:, c, h * 256 + j * 128:h * 256 + (j + 1) * 128], wm[:], start=True, stop=False),
                        r=[wm.name, ("VL", c, h)], w=[b_.name])
                    sc.pe(lambda b_=b_, h=h, j=j, cs=cs: nc.tensor.matmul(
                        b_[:, j * 128:(j + 1) * 128], Cd[:, h, j * 128:(j + 1) * 128], MQ[:, h, cs], start=False, stop=True),
                        r=[("Cd", h), ("MQ", h, blk)], w=[b_.name])
                sc.pe(lambda b_=b_, wm=wm: nc.tensor.matmul(b_[:, 256:384], ones_b[:], wm[:], start=True, stop=False),
                      r=[wm.name, "ones_b"], w=[b_.name])
                sc.pe(lambda b_=b_, h=h, cs=cs: nc.tensor.matmul(b_[:, 256:384], ndr[:, h, :], MQ[:, h, cs], start=False, stop=True),
                      r=[("ndr", h), ("MQ", h, blk)], w=[b_.name])
                sc.pe(lambda b_=b_, h=h, cs=cs: nc.tensor.matmul(b_[:, 384:512], SEL[:, h, :], G3[:, cs], start=True, stop=True),
                      r=[SEL.name, g3], w=[b_.name])
                sc.act(lambda b_=b_, fl=fl: nc.scalar.copy(fl[:], b_[:, 384:512]), r=[b_.name], w=[fl.name])
                sc.act(lambda b_=b_, dm_=dm_: nc.scalar.activation(out=dm_[:], in_=b_[:, 256:384], func=AF.Abs), r=[b_.name], w=[dm_.name])
                sc.dve(lambda fl=fl, dm_=dm_: nc.vector.tensor_tensor(dm_[:], dm_[:], fl[:], ALU.max),
                       r=[dm_.name, fl.name], w=[dm_.name])
                sc.dve(lambda dm_=dm_: nc.vector.reciprocal(dm_[:], dm_[:]), r=[dm_.name], w=[dm_.name])
                for j in range(2):
                    tn = tmpn[(2 * k + j) % 3]
                    sc.dve(lambda b_=b_, tn=tn, dm_=dm_, j=j: nc.vector.tensor_tensor(tn[:], b_[:, j * 128:(j + 1) * 128], dm_[:], ALU.mult),
                           r=[b_.name, dm_.name], w=[tn.name])
                    sc.pool(lambda tn=tn, ho=ho, og_=og_, h=h, j=j: nc.gpsimd.tensor_tensor(ho[:, 2 * h + j, :], tn[:], og_[:, 2 * h + j, :], ALU.mult),
                            r=[tn.name, og_.name], w=[(ho.name, h)])
                sc.pe(lambda h=h, cs=cs: nc.tensor.transpose(pT[:, 0:128], MK[:, h, cs], ident_b[:]), r=[("MK", h, blk), "ident_b"], w=[pT.name])
                sc.dve(lambda ku_=ku_, ucol=ucol: nc.vector.tensor_scalar(ku_[:], pT[:, 0:128], ucol, None, op0=ALU.mult),
                       r=[pT.name, u_tm.name], w=[ku_.name])
                sc.pe(lambda c_=c_, ku_=ku_, h=h, c=c: nc.tensor.matmul(c_[:, 0:256], ku_[:], VL[:, c, h * 256:(h + 1) * 256], start=True, stop=True),
                      r=[ku_.name, ("VL", c, h)], w=[c_.name])
                sc.pe(lambda c_=c_, ku_=ku_: nc.tensor.matmul(c_[:, 256:257], ku_[:], ones_b[:, 0:1], start=True, stop=True),
                      r=[ku_.name, "ones_b"], w=[c_.name])
                dcol = dec_rep[:, h * 16 + c:h * 16 + c + 1]
                sc.dve(lambda c_=c_, h=h, dcol=dcol: nc.vector.scalar_tensor_tensor(
                    out=Cst[:, h, :], in0=Cst[:, h, :], scalar=dcol, in1=c_[:, 0:257], op0=ALU.mult, op1=ALU.add),
                    r=[c_.name, ("Cst", h), dec_rep.name], w=[("Cst", h)])
                if c + 1 < NCH:
                    dnext = dec_rep[:, h * 16 + c + 1:h * 16 + c + 2]
                    sc.dve(lambda h=h, dnext=dnext: nc.vector.tensor_scalar(Cd[:, h, :], Cst[:, h, 0:256], dnext, None, op0=ALU.mult),
                           r=[("Cst", h), dec_rep.name], w=[("Cd", h)])
                    sc.dve(lambda h=h, dnext=dnext: nc.vector.tensor_scalar(ndf[:, h:h + 1], Cst[:, h, 256:257], dnext, None, op0=ALU.mult),
                           r=[("Cst", h), dec_rep.name], w=[("ndf", h)])
                    sc.dve(lambda h=h: nc.vector.tensor_scalar(ndr[:, h, :], ones_f[:], ndf[:, h:h + 1], None, op0=ALU.mult),
                           r=[("ndf", h), "ones_f"], w=[("ndr", h)])
                k += 1
            sc.dma("sp", lambda ho=ho, cs=cs: nc.sync.dma_start(out=oTv[:, 8:16, cs], in_=ho[:]),
                   [(ho.name, h) for h in range(4)], [("oT_d", 8 + c2) for c2 in range(8)], None)
        sc.barrier()


def _stage_even(self, l):
    self.mla(l)
    self.mlstm(l)
    self.out_proj(self.inp("ev_w_out", l // 2), l)


Builder.mlstm = _mlstm
Builder.stage_even = _stage_even
```
